# Optimizing a Trainium2 kernel written in Bass

```python
import math
import jax
import jax.numpy as jnp
from jax import lax
import numpy as np

D_MODEL = 1024
BATCH = 32
SEQ = 256
DEPTH = 4
DEC_BATCH = 8
DEC_SEQ = 1024
PAST_LEN = 512

GRID_W = 64
HEAD_DIM = 64
A_HEADS = 8
A_KV_HEADS = 4
B_HEADS = 8
MIX_WIDTH = (A_HEADS + B_HEADS) * HEAD_DIM
ATTN_IN = (A_HEADS + 2 * A_KV_HEADS + 3 * B_HEADS) * HEAD_DIM
Q_BLOCK = 128
ROPE_THETA = 10000.0
NA_WIN_H = 8
NA_WIN_W = 16
NA_QCOLS = 16
NA_KCOLS = NA_QCOLS + NA_WIN_W
GLA_HEADS = 4
GLA_DK = D_MODEL // 2 // GLA_HEADS
GLA_DV = D_MODEL // GLA_HEADS
GLA_KDIM = GLA_HEADS * GLA_DK
GLA_VDIM = GLA_HEADS * GLA_DV
GLA_RANK = 16
GLA_GATE_NORM = 16.0
GLA_CHUNK = 64
GLA_IN = 2 * GLA_KDIM + 2 * GLA_VDIM + 2 * GLA_RANK
D_FF = ((8 * D_MODEL // 3 + 127) // 128) * 128
FFN_RES = 0.5
N_MOD = 9
N_ATTN = (DEPTH + 1) // 2
N_GLA = DEPTH // 2
EPS = 1e-6

kernel_name = 'hybrid_flow_prefix_trunk_step'


def rms_norm(x, g):
    xf = x.astype(jnp.float32)
    y = xf * lax.rsqrt(jnp.mean(xf * xf, axis=-1, keepdims=True) + EPS)
    return (y * g.astype(jnp.float32)).astype(x.dtype)


def modulate(h, shift, scale):
    return h * (1 + scale) + shift


def adaln(cond, w, b):
    m = jax.nn.silu(cond) @ w + b
    return m.reshape(m.shape[0], 1, N_MOD, D_MODEL)


def swiglu(h, wg, wu, wd):
    return (jax.nn.silu(h @ wg) * (h @ wu)) @ wd


def ffn_sublayer(x, shift, scale, gate, g, wg, wu, wd):
    h = modulate(rms_norm(x, g), shift, scale)
    return x + FFN_RES * gate * swiglu(h, wg, wu, wd)


def split_last(a, sizes):
    return jnp.split(a, np.cumsum(sizes)[:-1].tolist(), axis=-1)


def axial_rope(x, t):
    half = x.shape[-1] // 2
    freqs = ROPE_THETA ** (-jnp.arange(0, half, 2, dtype=jnp.float32) / half)

    def rot(xp, pos):
        ang = pos.astype(jnp.float32)[:, None] * freqs[None, :]
        cos = jnp.concatenate([jnp.cos(ang), jnp.cos(ang)], -1)[:, None, :]
        sin = jnp.concatenate([jnp.sin(ang), jnp.sin(ang)], -1)[:, None, :]
        xf = xp.astype(jnp.float32)
        x1, x2 = jnp.split(xf, 2, axis=-1)
        return xf * cos + jnp.concatenate([-x2, x1], -1) * sin

    out = jnp.concatenate([rot(x[..., :half], t // GRID_W), rot(x[..., half:], t % GRID_W)], -1)
    return out.astype(x.dtype)


def blocked_attention(q, k, v):
    bsz, t, hq, d = q.shape
    hkv = k.shape[2]
    grp = hq // hkv
    nb = t // Q_BLOCK
    scale = d ** -0.5
    qb = q.reshape(bsz, nb, Q_BLOCK, hkv, grp, d).transpose(1, 0, 3, 4, 2, 5)
    kt = k.transpose(0, 2, 1, 3)
    vt = v.transpose(0, 2, 1, 3)

    def one_block(qi):
        s = jnp.einsum('bhgqd,bhkd->bhgqk', qi, kt).astype(jnp.float32) * scale
        p = jax.nn.softmax(s, axis=-1).astype(v.dtype)
        return jnp.einsum('bhgqk,bhkd->bhgqd', p, vt)

    o = lax.map(one_block, qb)
    return o.transpose(1, 0, 4, 2, 3, 5).reshape(bsz, t, hq, d)


def neighbourhood_attention(q, k, v, k_ctx, v_ctx, rel_bias):
    bsz, t, h, d = q.shape
    rows = t // GRID_W
    wh = min(NA_WIN_H, rows)
    ncb = GRID_W // NA_QCOLS
    scale = d ** -0.5
    r = jnp.arange(rows)
    row_start = jnp.clip(r - wh // 2, 0, rows - wh)
    row_idx = row_start[:, None] + jnp.arange(wh)
    qcol = jnp.arange(GRID_W).reshape(ncb, NA_QCOLS)
    blk_start = jnp.clip(jnp.arange(ncb) * NA_QCOLS - NA_WIN_W // 2, 0, GRID_W - NA_KCOLS)
    col_idx = blk_start[:, None] + jnp.arange(NA_KCOLS)
    win_start = jnp.clip(qcol - NA_WIN_W // 2, 0, GRID_W - NA_WIN_W)
    kc = col_idx[:, None, :]
    in_win = (kc >= win_start[..., None]) & (kc < win_start[..., None] + NA_WIN_W)

    def gather(a):
        g = jnp.take(a.reshape(bsz, rows, GRID_W, h, d), row_idx, axis=1)
        g = jnp.take(g, col_idx, axis=3)
        return g.transpose(0, 5, 1, 3, 2, 4, 6).reshape(bsz, h, rows, ncb, wh * NA_KCOLS, d)

    kg = gather(k)
    vg = gather(v)
    qg = q.reshape(bsz, rows, ncb, NA_QCOLS, h, d).transpose(0, 4, 1, 2, 3, 5)

    dr = row_idx - r[:, None] + NA_WIN_H - 1
    dc = jnp.clip(kc - qcol[..., None] + NA_WIN_W - 1, 0, 2 * NA_WIN_W - 2)
    bias = rel_bias[:, dr[:, None, None, :, None], dc[None, :, :, None, :]]
    bias = bias.reshape(h, rows, ncb, NA_QCOLS, wh * NA_KCOLS).astype(jnp.float32)
    mask = jnp.broadcast_to(in_win[None, :, :, None, :], (rows, ncb, NA_QCOLS, wh, NA_KCOLS))
    mask = mask.reshape(rows, ncb, NA_QCOLS, wh * NA_KCOLS)

    s_win = jnp.einsum('bhrnqd,bhrnkd->bhrnqk', qg, kg).astype(jnp.float32) * scale + bias[None]
    s_win = jnp.where(mask[None, None], s_win, -jnp.inf)
    s_ctx = jnp.einsum('bhrnqd,bkhd->bhrnqk', qg, k_ctx).astype(jnp.float32) * scale
    n_win = s_win.shape[-1]
    p = jax.nn.softmax(jnp.concatenate([s_win, s_ctx], axis=-1), axis=-1).astype(v.dtype)
    o = (jnp.einsum('bhrnqk,bhrnkd->bhrnqd', p[..., :n_win], vg)
         + jnp.einsum('bhrnqk,bkhd->bhrnqd', p[..., n_win:], v_ctx))
    return o.transpose(0, 2, 3, 4, 1, 5).reshape(bsz, t, h, d)


def attn_project(h, w_in):
    bsz, t, _ = h.shape
    parts = split_last(h @ w_in, [A_HEADS * HEAD_DIM, A_KV_HEADS * HEAD_DIM, A_KV_HEADS * HEAD_DIM,
                                  B_HEADS * HEAD_DIM, B_HEADS * HEAD_DIM, B_HEADS * HEAD_DIM])
    return [p.reshape(bsz, t, -1, HEAD_DIM) for p in parts]


def attn_merge(oa, ob, w_out):
    bsz, t = oa.shape[:2]
    return jnp.concatenate([oa.reshape(bsz, t, -1), ob.reshape(bsz, t, -1)], axis=-1) @ w_out


def attn_mixer_ctx(h, w_in, w_out, gq, gk):
    qa, ka, va, qb, kb, vb = attn_project(h, w_in)
    qa = rms_norm(qa, gq)
    ka = rms_norm(ka, gk)
    oa = blocked_attention(qa, ka, va)
    ob = blocked_attention(qb, kb, vb)
    return attn_merge(oa, ob, w_out), ka, va, kb, vb


def attn_mixer_lat(h, t_lat, ka_ctx, va_ctx, kb_ctx, vb_ctx, w_in, w_out, gq, gk, rel_bias):
    qa, ka, va, qb, kb, vb = attn_project(h, w_in)
    qa = axial_rope(rms_norm(qa, gq), t_lat)
    ka = axial_rope(rms_norm(ka, gk), t_lat)
    oa = blocked_attention(qa, jnp.concatenate([ka, ka_ctx], axis=1), jnp.concatenate([va, va_ctx], axis=1))
    ob = neighbourhood_attention(qb, kb, vb, kb_ctx, vb_ctx, rel_bias)
    return attn_merge(oa, ob, w_out)


def gla_chunked(q, k, v, log_g, s0):
    bsz, t, h, _ = q.shape
    n = t // GLA_CHUNK

    def chunks(a):
        return a.reshape(bsz, n, GLA_CHUNK, h, a.shape[-1]).transpose(1, 0, 3, 2, 4)

    qc, kc, vc, gc = chunks(q), chunks(k), chunks(v), chunks(log_g)
    b = jnp.cumsum(gc.astype(jnp.float32), axis=-2)
    b_last = b[..., -1:, :]
    q_in = qc.astype(jnp.float32) * jnp.exp(b)
    k_in = kc.astype(jnp.float32) * jnp.exp(-b)
    k_out = kc.astype(jnp.float32) * jnp.exp(b_last - b)
    causal = jnp.tril(jnp.ones((GLA_CHUNK, GLA_CHUNK), dtype=bool))
    a = jnp.where(causal, jnp.einsum('nbhqd,nbhkd->nbhqk', q_in, k_in), 0.0)
    o_intra = jnp.einsum('nbhqk,nbhkv->nbhqv', a, vc.astype(jnp.float32))
    u = jnp.einsum('nbhkd,nbhkv->nbhdv', k_out, vc.astype(jnp.float32))
    decay = jnp.exp(b_last[..., 0, :])

    def step(s, inp):
        q_i, dec_i, u_i = inp
        o_i = jnp.einsum('bhqd,bhdv->bhqv', q_i, s)
        return s * dec_i[..., None] + u_i, o_i

    s_fin, o_inter = lax.scan(step, s0.astype(jnp.float32), (q_in, decay, u))
    o = (o_intra + o_inter).transpose(1, 0, 3, 2, 4).reshape(bsz, t, h, -1)
    return o.astype(q.dtype), s_fin.astype(q.dtype)


def gla_mixer(h, s0_f, s0_b, w_in, w_gup, b_gup, g_norm, w_out):
    bsz, t, _ = h.shape
    q, k, v, g, d_f, d_b = split_last(h @ w_in, [GLA_KDIM, GLA_KDIM, GLA_VDIM, GLA_VDIM, GLA_RANK, GLA_RANK])
    lg_f = jax.nn.log_sigmoid((d_f @ w_gup[0] + b_gup[0]).astype(jnp.float32)) / GLA_GATE_NORM
    lg_b = jax.nn.log_sigmoid((d_b @ w_gup[1] + b_gup[1]).astype(jnp.float32)) / GLA_GATE_NORM
    heads_k = lambda a: a.reshape(bsz, t, GLA_HEADS, GLA_DK)
    q = heads_k(q) * (GLA_DK ** -0.5)
    k = heads_k(k)
    v = v.reshape(bsz, t, GLA_HEADS, GLA_DV)
    lg_f, lg_b = heads_k(lg_f), heads_k(lg_b)
    rev = lambda a: a[:, ::-1]
    o_f, s_f = gla_chunked(q, k, v, lg_f, s0_f)
    o_b, s_b = gla_chunked(rev(q), rev(k), rev(v), rev(lg_b), s0_b)
    o = rms_norm(o_f + rev(o_b), g_norm).reshape(bsz, t, GLA_VDIM) * jax.nn.silu(g)
    return o @ w_out, s_f, s_b


def setup_inputs(seed: int = 0) -> dict:
    key = jax.random.key(seed)
    ks = jax.random.split(key, 26)

    def nrm(k, shape, s=1.0):
        return jax.random.normal(k, shape, jnp.float32) * s

    def gain(k, shape):
        return 1.0 + nrm(k, shape, 0.05)

    return {
        'x_prompt': nrm(ks[0], (BATCH, SEQ, D_MODEL)),
        'x_sample': nrm(ks[1], (DEC_BATCH, DEC_SEQ, D_MODEL)),
        'cache_a_k': nrm(ks[2], (DEC_BATCH, N_ATTN, PAST_LEN, A_KV_HEADS, HEAD_DIM)),
        'cache_a_v': nrm(ks[3], (DEC_BATCH, N_ATTN, PAST_LEN, A_KV_HEADS, HEAD_DIM)),
        'cache_b_k': nrm(ks[4], (DEC_BATCH, N_ATTN, PAST_LEN, B_HEADS, HEAD_DIM)),
        'cache_b_v': nrm(ks[5], (DEC_BATCH, N_ATTN, PAST_LEN, B_HEADS, HEAD_DIM)),
        'state_gla': nrm(ks[6], (DEC_BATCH, N_GLA, 2, GLA_HEADS, GLA_DK, GLA_DV)),
        'c': nrm(ks[7], (DEC_BATCH, D_MODEL)),
        'c_ctx': nrm(ks[8], (D_MODEL,)),
        'w_mod': nrm(ks[9], (DEPTH, D_MODEL, N_MOD * D_MODEL), 0.5 * D_MODEL ** -0.5),
        'b_mod': nrm(ks[10], (DEPTH, N_MOD * D_MODEL), 0.02),
        'g_norm': gain(ks[11], (DEPTH, 3, D_MODEL)),
        'w_ffn_gate': nrm(ks[12], (DEPTH, 2, D_MODEL, D_FF), D_MODEL ** -0.5),
        'w_ffn_up': nrm(ks[13], (DEPTH, 2, D_MODEL, D_FF), D_MODEL ** -0.5),
        'w_ffn_down': nrm(ks[14], (DEPTH, 2, D_FF, D_MODEL), D_FF ** -0.5),
        'w_attn_in': nrm(ks[15], (N_ATTN, D_MODEL, ATTN_IN), D_MODEL ** -0.5),
        'w_attn_out': nrm(ks[16], (N_ATTN, MIX_WIDTH, D_MODEL), MIX_WIDTH ** -0.5),
        'g_qnorm': gain(ks[17], (N_ATTN, HEAD_DIM)),
        'g_knorm': gain(ks[18], (N_ATTN, HEAD_DIM)),
        'na_rel_bias': nrm(ks[19], (N_ATTN, B_HEADS, 2 * NA_WIN_H - 1, 2 * NA_WIN_W - 1), 0.1),
        'w_gla_in': nrm(ks[20], (N_GLA, D_MODEL, GLA_IN), D_MODEL ** -0.5),
        'w_gla_gup': nrm(ks[21], (N_GLA, 2, GLA_RANK, GLA_KDIM), GLA_RANK ** -0.5),
        'b_gla_gup': nrm(ks[22], (N_GLA, 2, GLA_KDIM), 0.1),
        'g_gla_norm': gain(ks[23], (N_GLA, GLA_DV)),
        'w_gla_out': nrm(ks[24], (N_GLA, GLA_VDIM, D_MODEL), GLA_VDIM ** -0.5),
        'g_final': gain(ks[25], (D_MODEL,)),
    }


def reference(x_prompt, x_sample, cache_a_k, cache_a_v, cache_b_k, cache_b_v, state_gla, c, c_ctx,
              w_mod, b_mod, g_norm, w_ffn_gate, w_ffn_up, w_ffn_down, w_attn_in, w_attn_out,
              g_qnorm, g_knorm, na_rel_bias, w_gla_in, w_gla_gup, b_gla_gup, g_gla_norm, w_gla_out, g_final):
    t_lat = jnp.arange(x_sample.shape[1])
    xp, xs = x_prompt, x_sample
    new_a_k, new_a_v, new_b_k, new_b_v, new_gla = [], [], [], [], []
    for l in range(DEPTH):
        mp = adaln(c_ctx[None, :], w_mod[l], b_mod[l])
        ms = adaln(c, w_mod[l], b_mod[l])
        xp = ffn_sublayer(xp, mp[:, :, 0], mp[:, :, 1], mp[:, :, 2], g_norm[l, 0],
                          w_ffn_gate[l, 0], w_ffn_up[l, 0], w_ffn_down[l, 0])
        xs = ffn_sublayer(xs, ms[:, :, 0], ms[:, :, 1], ms[:, :, 2], g_norm[l, 0],
                          w_ffn_gate[l, 0], w_ffn_up[l, 0], w_ffn_down[l, 0])
        hp = modulate(rms_norm(xp, g_norm[l, 1]), mp[:, :, 3], mp[:, :, 4])
        hs = modulate(rms_norm(xs, g_norm[l, 1]), ms[:, :, 3], ms[:, :, 4])
        if l % 2 == 0:
            i = l // 2
            op, ka, va, kb, vb = attn_mixer_ctx(hp, w_attn_in[i], w_attn_out[i], g_qnorm[i], g_knorm[i])
            os_ = attn_mixer_lat(hs, t_lat, cache_a_k[:, i], cache_a_v[:, i], cache_b_k[:, i], cache_b_v[:, i],
                                 w_attn_in[i], w_attn_out[i], g_qnorm[i], g_knorm[i], na_rel_bias[i])
            new_a_k.append(ka)
            new_a_v.append(va)
            new_b_k.append(kb)
            new_b_v.append(vb)
        else:
            j = l // 2
            zero = jnp.zeros((xp.shape[0], GLA_HEADS, GLA_DK, GLA_DV), xp.dtype)
            op, s_f, s_b = gla_mixer(hp, zero, zero, w_gla_in[j], w_gla_gup[j], b_gla_gup[j],
                                     g_gla_norm[j], w_gla_out[j])
            os_, _, _ = gla_mixer(hs, state_gla[:, j, 0], state_gla[:, j, 1], w_gla_in[j], w_gla_gup[j],
                                  b_gla_gup[j], g_gla_norm[j], w_gla_out[j])
            new_gla.append(jnp.stack([s_f, s_b], axis=1))
        xp = xp + mp[:, :, 5] * op
        xs = xs + ms[:, :, 5] * os_
        xp = ffn_sublayer(xp, mp[:, :, 6], mp[:, :, 7], mp[:, :, 8], g_norm[l, 2],
                          w_ffn_gate[l, 1], w_ffn_up[l, 1], w_ffn_down[l, 1])
        xs = ffn_sublayer(xs, ms[:, :, 6], ms[:, :, 7], ms[:, :, 8], g_norm[l, 2],
                          w_ffn_gate[l, 1], w_ffn_up[l, 1], w_ffn_down[l, 1])
    y_prompt = rms_norm(xp, g_final)
    y_sample = rms_norm(xs, g_final)
    return (y_prompt, y_sample, jnp.stack(new_a_k, axis=1), jnp.stack(new_a_v, axis=1),
            jnp.stack(new_b_k, axis=1), jnp.stack(new_b_v, axis=1), jnp.stack(new_gla, axis=1))
```

```python
import math
import os
import numpy as np
KDBG = int(os.environ.get('KDBG', '99'))
import concourse.bass as bass
import concourse.mybir as mybir
from concourse.bass_utils import run_bass_kernel_spmd

F32 = mybir.dt.float32
BF16 = mybir.dt.bfloat16
AF = mybir.ActivationFunctionType
ALU = mybir.AluOpType
AX = mybir.AxisListType

D = 1024
NT = 2048
DFF = 2816
NF = DFF // 128
EPS = 1e-6
N_CORES = 8
DEPTH = 4
NEG = -30000.0


_PHASE = {"list": None}


class Buf:
    __slots__ = ("name", "w", "r", "sem", "excl")

    def __init__(self, name, excl=False):
        self.name = name
        self.excl = excl
        self.w = None
        self.r = []
        self.sem = {}
        if _PHASE["list"] is not None:
            _PHASE["list"].append(self)


class Prog:
    ENG = ("pe", "act", "dve", "pool", "sp")

    def __init__(self):
        self.prog = {e: [] for e in self.ENG}
        self.cnt = {e: 0 for e in self.ENG}
        self.seen = {e: {} for e in self.ENG}
        self.nsem = len(self.ENG)
        self.semtot = {}
        self.free = {"hw": [], "sw": []}
        self.pending = {e: [] for e in self.ENG}

    def semid(self, e):
        return self.ENG.index(e)

    def _need(self, eng, dep, out):
        sem, val, _ = dep
        if out.get(sem, 0) < val:
            out[sem] = val

    def _deps(self, eng, reads, writes, skip_sem=None, is_dma=False):
        out = {}
        for d in self.pending[eng]:
            self._need(eng, d, out)
        self.pending[eng] = []
        for b in reads:
            if b.w is not None:
                self._need(eng, b.w, out)
            if b.excl:
                for d in b.r:
                    if d[2] != eng:
                        self._need(eng, d, out)
        for b in writes:
            strict = is_dma or eng == "pool"
            if b.w is not None and (strict or b.w[2] != eng) and not (b.w[2] == "dma" and b.w[0] == skip_sem):
                self._need(eng, b.w, out)
            for d in b.r:
                if strict or d[2] != eng:
                    self._need(eng, d, out)
        res = []
        for sem, val in out.items():
            if self.seen[eng].get(sem, 0) >= val:
                continue
            self.seen[eng][sem] = val
            res.append((sem, val))
        return res

    def op(self, eng, fn, reads=(), writes=(), strict=False):
        waits = self._deps(eng, reads, writes, is_dma=strict)
        self.cnt[eng] += 1
        idx = self.cnt[eng]
        sem = self.semid(eng)
        self.prog[eng].append((waits, fn, sem, 1))
        for b in reads:
            b.r.append((sem, idx, eng))
        for b in writes:
            b.w = (sem, idx, eng)
            b.r = []

    def dma(self, q, out, in_, dst=None, src=None):
        owner = dst if dst is not None else src
        cls = "sw" if q == "pool" else "hw"
        if cls not in owner.sem:
            if self.free[cls]:
                owner.sem[cls] = self.free[cls].pop()
            else:
                owner.sem[cls] = self.nsem
                self.nsem += 1
                self.semtot[owner.sem[cls]] = 0
        sm = owner.sem[cls]
        waits = self._deps(q, [src] if src is not None else [], [dst] if dst is not None else [], skip_sem=sm, is_dma=True)
        self.semtot[sm] += 16
        tot = self.semtot[sm]
        self.prog[q].append((waits, (lambda e, o=out, i=in_: e.dma_start(out=o, in_=i)), sm, 16))
        if src is not None:
            src.r.append((sm, tot, "dma"))
        if dst is not None:
            dst.r = []
            dst.w = (sm, tot, "dma")

    def barrier(self):
        deps = [(self.semid(e), self.cnt[e], "bar") for e in self.ENG if self.cnt[e] > 0]
        deps += [(sm, tot, "bar") for sm, tot in self.semtot.items() if tot > 0]
        for e in self.ENG:
            self.pending[e] = list(deps)

    def phase_begin(self):
        _PHASE["list"] = []

    def phase_end(self):
        self.barrier()
        for b in _PHASE["list"]:
            for cls, sm in b.sem.items():
                self.free[cls].append(sm)
            b.sem = {}
        _PHASE["list"] = None

    def replay(self, eng, e, sems):
        for waits, fn, sem, inc in self.prog[eng]:
            for (ws, wv) in waits:
                e.wait_ge(sems[ws], wv)
            ins = fn(e)
            ins.then_inc(sems[sem], inc)


def _constants():
    c = {}
    c["k_ident"] = np.eye(128, dtype=np.float32)
    c["k_ones"] = np.ones((128, 128), np.float32)
    bo = np.zeros((128, 128), np.float32)
    bo[:64, :64] = 1.0
    bo[64:, 64:] = 1.0
    c["k_bones"] = bo
    R = np.zeros((128, 128), np.float32)
    for m in range(128):
        blk = (m // 32) * 32
        j = m % 32
        if j < 16:
            R[blk + j + 16, m] = -1.0
        else:
            R[blk + j - 16, m] = 1.0
    c["k_rperm"] = R
    t = np.arange(1024)
    freqs = 10000.0 ** (-np.arange(0, 32, 2, dtype=np.float64) / 32.0)
    cosT = np.zeros((128, 1024), np.float64)
    sinT = np.zeros((128, 1024), np.float64)
    for p in range(128):
        d = p % 64
        pos = (t // 64) if d < 32 else (t % 64)
        f = freqs[(d % 32) % 16]
        ang = (pos.astype(np.float32) * np.float32(f)).astype(np.float64)
        cosT[p] = np.cos(ang)
        sinT[p] = np.sin(ang)
    c["k_cos"] = cosT.astype(np.float32)
    c["k_sin"] = sinT.astype(np.float32)
    s = np.arange(128)[:, None]
    tt = np.arange(128)[None, :]
    same = (s // 128) == (tt // 128)
    maskf = (same & (s <= tt)).astype(np.float32)
    maskb = (same & (s >= tt)).astype(np.float32)
    suf = (same & (s > tt)).astype(np.float32)
    sub = (same & (s < tt)).astype(np.float32)
    c["k_tri"] = np.stack([maskf, maskb, -maskf / 16.0, -maskb / 16.0, -suf / 16.0, -sub / 16.0]).astype(np.float32)
    op = np.zeros((128, 2, 128), np.float32)
    op[:, 0, :64] = 1.0
    op[:, 1, 64:] = 1.0
    c["k_opad"] = op.reshape(128, 256)
    kc = np.arange(64)[:, None]
    qc = np.arange(64)[None, :]
    dc = kc - qc + 15
    ws = np.clip(qc - 8, 0, 48)
    inwin = (kc >= ws) & (kc < ws + 16)
    oh = np.zeros((31, 64, 64), np.float32)
    for b in range(31):
        oh[b] = np.where((dc == b) & inwin, 8.0, 0.0)
    c["k_oh8"] = oh.reshape(31, 4096)
    c["k_cmask"] = np.where(inwin, 0.0, NEG).astype(np.float32).reshape(1, 4096)
    perm = np.zeros((120, 120), np.float32)
    for h in range(8):
        for a in range(15):
            perm[h * 15 + a, h * 15 + (14 - a)] = 1.0
    c["k_perm"] = perm
    return c


CONST_SHAPES = {
    "k_ident": (128, 128), "k_ones": (128, 128), "k_bones": (128, 128), "k_rperm": (128, 128),
    "k_cos": (128, 1024), "k_sin": (128, 1024), "k_tri": (6, 128, 128), "k_opad": (128, 256),
    "k_oh8": (31, 4096), "k_cmask": (1, 4096), "k_perm": (120, 120),
}

IN_SHAPES = {
    "xp": (1024, 1024), "xs": (1024, 1024),
    "cak": (2, 512, 256), "cav": (2, 512, 256), "cbk": (2, 512, 512), "cbv": (2, 512, 512),
    "sgla": (2, 2, 4, 128, 256), "cond": (2, 1024),
    "w_mod": (4, 1024, 9216), "b_mod": (4, 9216), "g_norm": (4, 3, 1024),
    "wg": (4, 2, 1024, DFF), "wu": (4, 2, 1024, DFF), "wd": (4, 2, DFF, 1024),
    "w_ain": (2, 1024, 2560), "w_aout": (2, 1024, 1024), "gq": (2, 64), "gk": (2, 64),
    "relb": (2, 8, 15, 31),
    "w_gin": (2, 1024, 3104), "w_gup": (2, 2, 16, 512), "b_gup": (2, 2, 512), "g_gn": (2, 256),
    "w_gout": (2, 1024, 1024), "g_fin": (1024,),
}
OUT_SHAPES = {
    "yp": (1024, 1024), "ys": (1024, 1024),
    "nak": (4, 2, 256, 256), "nav": (4, 2, 256, 256), "nbk": (4, 2, 256, 512), "nbv": (4, 2, 256, 512),
    "ngla": (4, 2, 2, 4, 128, 256),
}


def build_program(n_layers=DEPTH, do_ffn=True, do_mix=True, dbg=False, groups=(0, 1), units=None):
    nc = bass.Bass("TRN2", target_bir_lowering=False)
    P = Prog()
    I = {k: nc.dram_tensor(k, list(v), F32, kind="ExternalInput").ap() for k, v in IN_SHAPES.items()}
    K = {k: nc.dram_tensor(k, list(v), F32, kind="ExternalInput").ap() for k, v in CONST_SHAPES.items()}
    O = {k: nc.dram_tensor(k, list(v), F32, kind="ExternalOutput").ap() for k, v in OUT_SHAPES.items()}
    if dbg:
        O["dbg"] = nc.dram_tensor("dbg", [128, 8, NT], F32, kind="ExternalOutput").ap()
    tabscr = nc.dram_tensor("tabscr", [2, 120, 4096], BF16, kind="Internal").ap()
    tabB = [Buf("tabscr0"), Buf("tabscr1")]

    SB_LO = 16512
    SB_HI = int(nc.sbuf_top) if not callable(nc.sbuf_top) else int(nc.sbuf_top())
    st = {"p": SB_LO, "n": 0}

    def alloc(shape, dt, name="t"):
        nb = 1
        for s_ in shape[1:]:
            nb *= s_
        nb *= 2 if dt == BF16 else 4
        nb = (nb + 31) // 32 * 32
        off = st["p"]
        st["p"] += nb
        assert st["p"] <= SB_HI, f"SBUF overflow at {name}: {st['p']} > {SB_HI}"
        st["n"] += 1
        return nc.alloc_sbuf_tensor_at(f"{name}_{st['n']}", list(shape), dt, offset=off).ap()

    def T(shape, dt, name="t"):
        return alloc(shape, dt, name), Buf(name)

    xT = alloc([128, 8, NT], F32, "xT")
    xB = [[Buf(f"x{g}{t}") for t in range(2)] for g in range(2)]
    hT = alloc([128, 8, NT], BF16, "hT")
    hB = [[Buf(f"h{g}{t}") for t in range(2)] for g in range(2)]
    ident, identB = T([128, 128], F32, "ident")
    identbf, identbfB = T([128, 128], BF16, "identbf")
    onesbf, onesbfB = T([128, 128], BF16, "onesbf")
    bones, bonesB = T([128, 128], BF16, "bones")
    rperm, rpermB = T([128, 128], BF16, "rperm")
    tri, triB = T([128, 6, 128], F32, "tri")
    opad, opadB = T([128, 2, 128], BF16, "opad")
    onesf, onesfB = T([1, 128], F32, "onesf")
    cst, cstB = T([128, 4], F32, "cst")
    MODS = []
    for i_ in range(2):
        MODS.append({"mod": T([128, 72, 2], F32, f"mod{i_}"), "acoef": T([128, 3, 8, 2], F32, f"acoef{i_}"),
                     "gcoef": T([128, 3, 8, 2], F32, f"gcoef{i_}")})
    CUR = {"l": 0}
    gnt, gntB = T([128, 24], F32, "gnt")
    bmt, bmtB = T([128, 72], F32, "bmt")
    scT, scB = T([128, 2, 8], BF16, "scT")
    condt, condB = T([128, 16], F32, "condt")
    vstg, vstgB = T([72, 128], F32, "vstg")
    wst = [T([128, 8, 512], BF16, f"wst{i}") for i in range(3)]
    NORM_BASE = st["p"]
    sqb, sqbB = T([128, 8, 512], BF16, "sqb")
    rstd, rstdB = T([128, 512], F32, "rstd")
    lnt, lntB = T([128, 512], F32, "lnt")
    ntmp = [T([128, 512], F32, f"ntmp{i}") for i in range(3)]
    PH_BASE = st["p"]

    ps = [nc.alloc_psum_tensor(f"ps{i}", [128, 512], F32).ap() for i in range(8)]
    psB = [Buf(f"ps{i}", excl=True) for i in range(8)]
    rot_state = {"banks": list(range(8)), "i": 0}

    def rot():
        b = rot_state["banks"][rot_state["i"] % len(rot_state["banks"])]
        rot_state["i"] += 1
        return ps[b], psB[b]

    def set_rot(banks):
        rot_state["banks"] = list(banks)
        rot_state["i"] = 0

    cyc = {}

    def nxt(key, lst):
        i = cyc.get(key, 0)
        cyc[key] = i + 1
        return lst[i % len(lst)]

    def mm_group(out, pairs):
        def f(e):
            n = len(pairs)
            ins = None
            for i, (l, r) in enumerate(pairs):
                ins = e.matmul(out, lhsT=l, rhs=r, start=(i == 0), stop=(i == n - 1))
            return ins
        return f

    def mm_one(out, l, r, start, stop):
        return lambda e: e.matmul(out, lhsT=l, rhs=r, start=start, stop=stop)

    def tr_one(out, in_, idn):
        return lambda e: e.transpose(out, in_, idn)

    def tr_many(items):
        def f(e):
            ins = None
            for (o, i_, idn) in items:
                ins = e.transpose(o, i_, idn)
            return ins
        return f

    def act(out, in_, func, bias=None, scale=None):
        kw = {}
        if bias is not None:
            kw["bias"] = bias
        if scale is not None:
            kw["scale"] = scale
        return lambda e: e.activation(out=out, in_=in_, func=func, **kw)

    def tt(out, a, b, op):
        return lambda e: e.tensor_tensor(out=out, in0=a, in1=b, op=op)

    def stt(out, in0, scalar, in1, op0, op1):
        return lambda e: e.scalar_tensor_tensor(out=out, in0=in0, scalar=scalar, in1=in1, op0=op0, op1=op1)

    def tsm(out, in0, s1):
        return lambda e: e.tensor_scalar_mul(out=out, in0=in0, scalar1=s1)

    def cp(out, in_):
        return lambda e: e.tensor_copy(out=out, in_=in_)

    def mset(ap, v):
        return lambda e: e.memset(ap, v)

    def recip(out, in_):
        return lambda e: e.reciprocal(out=out, in_=in_)

    P.dma("sp", ident, K["k_ident"], dst=identB)
    P.dma("pool", identbf, K["k_ident"], dst=identbfB)
    P.dma("pool", onesbf, K["k_ones"], dst=onesbfB)
    P.dma("pool", bones, K["k_bones"], dst=bonesB)
    P.dma("pool", rperm, K["k_rperm"], dst=rpermB)
    P.dma("sp", tri, K["k_tri"].rearrange("a p n -> p a n"), dst=triB)
    P.dma("pool", opad, K["k_opad"].rearrange("p (a n) -> p a n", a=2), dst=opadB)
    P.dma("sp", onesf, K["k_ones"][0:1, :], dst=onesfB)
    P.op("dve", mset(cst[:, 0:1], EPS), writes=[cstB])
    P.op("dve", mset(cst[:, 1:2], 1.0), writes=[cstB])
    P.op("dve", mset(cst[:, 2:3], math.log(128.0 ** -0.5)), writes=[cstB])
    P.op("dve", mset(cst[:, 3:4], 0.0), writes=[cstB])
    c_eps, c_one, c_lnq = cst[:, 0:1], cst[:, 1:2], cst[:, 2:3]

    def load_fm(src2d, n, dst, dstB):
        P.dma("sp", vstg[0:n, :], src2d, dst=vstgB)
        p_, pb = rot()
        P.op("pe", tr_one(p_[:, 0:n], vstg[0:n, :], ident[0:n, 0:n]), reads=[vstgB, identB], writes=[pb])
        P.op("dve", cp(dst, p_[:, 0:n]), reads=[pb], writes=[dstB])

    P.phase_begin()
    st["p"] = PH_BASE
    xin = [T([128, 1024], F32, f"xin{i}") for i in range(3)]
    for tt_ in range(16):
        g, tl = tt_ // 8, tt_ % 8
        src = (I["xp"] if g == 0 else I["xs"])[tl * 128:(tl + 1) * 128, :]
        xi, xiB = nxt("xin", xin)
        P.dma("sp", xi, src, dst=xiB)
        c0 = g * 1024 + tl * 128
        for half in range(2):
            p_, pb = rot()
            P.op("pe", tr_many([(p_[:, j * 128:(j + 1) * 128], xi[:, (half * 4 + j) * 128:(half * 4 + j + 1) * 128], ident)
                                for j in range(4)]), reads=[xiB, identB], writes=[pb])
            eng = "dve" if half == 0 else "act"
            dst = xT[:, half * 4:half * 4 + 4, c0:c0 + 128]
            srcp = p_.rearrange("p (a b) -> p a b", a=4)
            if eng == "dve":
                P.op("dve", cp(dst, srcp), reads=[pb], writes=[xB[g][tl // 4]])
            else:
                P.op("act", act(dst, srcp, AF.Copy), reads=[pb], writes=[xB[g][tl // 4]])

    P.phase_end()
    load_fm(I["cond"].rearrange("j (c p) -> (j c) p", p=128), 16, condt, condB)
    P.op("act", act(scT.rearrange("p j k -> p (j k)"), condt, AF.Silu), reads=[condB], writes=[scB])

    def adaln_steps(l, slots, pm, pmB, cgs=range(36), fin=(0, 1, 2)):
        modt, modB = MODS[l % 2]["mod"]
        acoef, acoefB = MODS[l % 2]["acoef"]
        gcoef, gcoefB = MODS[l % 2]["gcoef"]
        load_fm(I["b_mod"][l].rearrange("(c p) -> c p", p=128), 72, bmt, bmtB)
        load_fm(I["g_norm"][l].rearrange("s (c p) -> (s c) p", p=128), 24, gnt, gntB)
        wsrc = I["w_mod"][l].rearrange("(k p) n -> p k n", p=128)
        pending = []
        for cg in cgs:
            w_, wB = slots[cg % len(slots)]
            P.dma("pool", w_[:, :, 0:256], wsrc[:, :, cg * 256:(cg + 1) * 256], dst=wB)
            pending.append((cg, w_, wB))
            if len(pending) < len(slots):
                continue
            yield
            cg0, w0, w0B = pending.pop(0)
            for j in range(2):
                ch = cg0 * 2 + j
                P.op("pe", mm_group(pm[:, ch * 2:ch * 2 + 2],
                                    [(w0[:, k, j * 128:(j + 1) * 128], scT[:, :, k]) for k in range(8)]),
                     reads=[w0B, scB], writes=[pmB])
        for (cg0, w0, w0B) in pending:
            for j in range(2):
                ch = cg0 * 2 + j
                P.op("pe", mm_group(pm[:, ch * 2:ch * 2 + 2],
                                    [(w0[:, k, j * 128:(j + 1) * 128], scT[:, :, k]) for k in range(8)]),
                     reads=[w0B, scB], writes=[pmB])
        pmv = pm[:, 0:144].rearrange("p (c j) -> p c j", j=2)
        for s in fin:
            for j in range(2):
                P.op("dve", tt(modt[:, 24 * s:24 * s + 24, j], pmv[:, 24 * s:24 * s + 24, j], bmt[:, 24 * s:24 * s + 24], ALU.add),
                     reads=[pmB, bmtB], writes=[modB])
        for s in fin:
            for j in range(2):
                P.op("dve", stt(acoef[:, s, :, j], modt[:, (3 * s + 1) * 8:(3 * s + 2) * 8, j], 1.0,
                                gnt[:, s * 8:(s + 1) * 8], ALU.add, ALU.mult), reads=[modB, gntB], writes=[acoefB])
                P.op("dve", tsm(gcoef[:, s, :, j], modt[:, (3 * s + 2) * 8:(3 * s + 3) * 8, j],
                                0.5 if s != 1 else 1.0), reads=[modB], writes=[gcoefB])
        yield

    def adaln_now(l, cgs=range(36), fin=(0, 1, 2)):
        for _ in adaln_steps(l, wst, ps[0], psB[0], cgs, fin):
            pass

    def rstd_from(ss_ap, ssB, scale, out_ap, outB, tmp_ap, tmpB):
        P.op("act", act(tmp_ap, ss_ap, AF.Ln, bias=c_eps, scale=scale), reads=[ssB, cstB], writes=[tmpB])
        P.op("act", act(out_ap, tmp_ap, AF.Exp, scale=-0.5), reads=[tmpB], writes=[outB])

    def norm_mod(s, g):
        for t in range(2):
            c0 = g * 1024 + t * 512
            xb, hb = xB[g][t], hB[g][t]
            P.op("act", act(sqb, xT[:, :, c0:c0 + 512], AF.Square), reads=[xb], writes=[sqbB])
            p_, pb = rot()
            P.op("pe", mm_group(p_, [(onesbf, sqb[:, k, :]) for k in range(8)]), reads=[sqbB, onesbfB], writes=[pb])
            rstd_from(p_, pb, 1.0 / D, rstd, rstdB, lnt, lntB)
            for k in range(8):
                tm, tmB = nxt("ntmp", ntmp)
                P.op("dve", tt(tm, xT[:, k, c0:c0 + 512], rstd, ALU.mult), reads=[xb, rstdB], writes=[tmB])
                modt, modB = MODS[CUR["l"] % 2]["mod"]
                acoef, acoefB = MODS[CUR["l"] % 2]["acoef"]
                P.op("act", act(hT[:, k, c0:c0 + 512], tm, AF.Identity, bias=modt[:, 3 * s * 8 + k, g:g + 1],
                                scale=acoef[:, s, k, g:g + 1]), reads=[tmB, modB, acoefB], writes=[hb])

    def x_update(p_ap, pb, d, c0, n, s, g, xb, eng="dve"):
        xa = xT[:, d, c0:c0 + n]
        gcoef, gcoefB = MODS[CUR["l"] % 2]["gcoef"]
        P.op(eng, stt(xa, p_ap, gcoef[:, s, d, g:g + 1], xa, ALU.mult, ALU.add), reads=[pb, gcoefB, xb], writes=[xb])

    def ffn(l, idx, s, bg_layer=None, bg_cgs=None, bg_fin=None, bg_rate=2):
        P.phase_begin()
        st["p"] = PH_BASE
        parts = [(0, 5), (5, 10), (10, 14), (14, 18), (18, 22)]
        actT = alloc([128, 5, NT], BF16, "act")
        actB = [Buf(f"act{t}") for t in range(4)]
        wdt = [T([128, 5, 1024], BF16, f"wd{i}") for i in range(2)]
        sgt = [T([128, 512], BF16, f"sg{i}") for i in range(3)]
        bg = None
        if bg_layer is not None:
            wms = [T([128, 8, 256], BF16, f"wm{i}") for i in range(4)]
            set_rot(range(1, 8))
            bg = adaln_steps(bg_layer, wms, ps[0], psB[0], bg_cgs, bg_fin)
        else:
            set_rot(range(8))
        norm_mod(s, 0)
        norm_mod(s, 1)
        wgs = I["wg"][l, idx].rearrange("(k p) n -> p k n", p=128)
        wus = I["wu"][l, idx].rearrange("(k p) n -> p k n", p=128)
        wds = I["wd"][l, idx].rearrange("(f p) n -> p f n", p=128)
        for (f0, f1) in parts:
            nf = f1 - f0
            wd_, wdB = nxt("wd", wdt)
            P.dma("pool", wd_[:, 0:nf, :], wds[:, f0:f1, :], dst=wdB)
            fs = f0
            while fs < f1:
                nsub = min(2, f1 - fs)
                w_, wB = nxt("wst", wst)
                P.dma("pool", w_[:, :, 0:nsub * 128], wgs[:, :, fs * 128:(fs + nsub) * 128], dst=wB)
                P.dma("pool", w_[:, :, 256:256 + nsub * 128], wus[:, :, fs * 128:(fs + nsub) * 128], dst=wB)
                for fi in range(nsub):
                    fl = fs + fi - f0
                    for t in range(4):
                        c0 = t * 512
                        hb = hB[t // 2][t % 2]
                        pg, pgB = rot()
                        P.op("pe", mm_group(pg, [(w_[:, k, fi * 128:(fi + 1) * 128], hT[:, k, c0:c0 + 512]) for k in range(8)]),
                             reads=[wB, hb], writes=[pgB])
                        pu, puB = rot()
                        P.op("pe", mm_group(pu, [(w_[:, k, 256 + fi * 128:256 + (fi + 1) * 128], hT[:, k, c0:c0 + 512]) for k in range(8)]),
                             reads=[wB, hb], writes=[puB])
                        sg, sgB = nxt("sg", sgt)
                        P.op("act", act(sg, pg, AF.Silu), reads=[pgB], writes=[sgB])
                        P.op("dve", tt(actT[:, fl, c0:c0 + 512], sg, pu, ALU.mult), reads=[sgB, puB], writes=[actB[t]])
                    if bg is not None:
                        for _ in range(bg_rate):
                            next(bg, None)
                fs += nsub
            for t in range(4):
                c0 = t * 512
                g = t // 2
                for d in range(8):
                    p_, pb = rot()
                    P.op("pe", mm_group(p_, [(wd_[:, fl, d * 128:(d + 1) * 128], actT[:, fl, c0:c0 + 512]) for fl in range(nf)]),
                         reads=[wdB, actB[t]], writes=[pb])
                    x_update(p_, pb, d, c0, 512, s, g, xB[g][t % 2])
        if bg is not None:
            for _ in bg:
                pass
        P.phase_end()

    def attn_build_tab(i):
        P.phase_begin()
        st["p"] = PH_BASE
        relt, relB = T([120, 31], F32, "relt")
        relbf, relbfB = T([120, 31], BF16, "relbf")
        perm, permB = T([120, 120], BF16, "perm")
        btr, btrB = T([31, 120], BF16, "btr")
        oh8, oh8B = T([31, 4096], BF16, "oh8")
        cmask, cmaskB = T([1, 4096], BF16, "cmask")
        ones1, ones1B = T([1, 128], BF16, "ones1")
        tabsb, tabsbB = T([120, 4096], BF16, "tabsb")
        P.dma("sp", relt, I["relb"][i].rearrange("h a b -> (h a) b"), dst=relB)
        P.dma("pool", perm, K["k_perm"], dst=permB)
        P.dma("pool", oh8, K["k_oh8"], dst=oh8B)
        P.dma("pool", cmask, K["k_cmask"], dst=cmaskB)
        P.dma("pool", ones1, K["k_ones"][0:1, :], dst=ones1B)
        P.op("dve", cp(relbf, relt), reads=[relB], writes=[relbfB])
        p_, pb = rot()
        P.op("pe", mm_one(p_[0:31, 0:120], relbf, perm, True, True), reads=[relbfB, permB], writes=[pb])
        P.op("dve", cp(btr, p_[0:31, 0:120]), reads=[pb], writes=[btrB])
        for cgi in range(8):
            p_, pb = rot()
            cs = slice(cgi * 512, (cgi + 1) * 512)
            P.op("pe", mm_group(p_[0:120, :], [(btr, oh8[:, cs]), (ones1[0:1, 0:120], cmask[0:1, cs])]),
                 reads=[btrB, oh8B, ones1B, cmaskB], writes=[pb])
            P.op("act" if cgi % 2 else "dve", (act(tabsb[:, cs], p_[0:120, :], AF.Copy) if cgi % 2 else cp(tabsb[:, cs], p_[0:120, :])),
                 reads=[pb], writes=[tabsbB])
        P.dma("sp", tabscr[i], tabsb, dst=tabB[i], src=tabsbB)
        P.phase_end()

    def attn_group(l, g):
        i = l // 2
        s = 1
        P.phase_begin()
        st["p"] = PH_BASE
        set_rot([4, 5, 6, 7])
        norm_mod(s, g)
        P.barrier()
        st["p"] = NORM_BASE
        NK = 12 if g == 1 else 8
        sets = []
        for i_ in range(2):
            sets.append({"q": T([128, 1024], BF16, f"qT{i_}"), "k": T([128, NK * 128], BF16, f"kT{i_}"),
                         "v": T([128, 2, NK, 128], BF16, f"vp{i_}"), "on": T([128, 1024], BF16, f"onT{i_}")})
        pbuf = [T([128, 512], BF16, f"pb{i_}") for i_ in range(4)]
        rden, rdenB = T([128, 512], F32, "rden")
        wo_ = [T([128, 1024], BF16, f"wo{i_}") for i_ in range(2)]
        gqk, gqkB = T([128, 2], F32, "gqk")
        hsq, hsqB = T([128, 512], BF16, "hsq")
        hr, hrB = T([128, 512], F32, "hr")
        if g == 0:
            xtmp = [T([128, 512], F32, f"xtmp{i_}") for i_ in range(2)]
            kf32, kf32B = T([128, 512], F32, "kf32")
            ostg, ostgB = T([128, 8, 128], F32, "ostg")
            vst2, vst2B = T([128, 8, 128], F32, "vst2")
        else:
            qnb, qnbB = T([128, 512], BF16, "qnb")
            rt1, rt1B = T([128, 512], F32, "rt1")
            rt2, rt2B = T([128, 512], F32, "rt2")
            kcs, kcsB = T([128, 4, 128], F32, "kcs")
            btab = [T([128, 6, 512], BF16, f"btab{i_}") for i_ in range(4)]
            for i_ in range(4):
                P.op("pool", mset(btab[i_][0].rearrange("p a b -> p (a b)"), NEG), writes=[btab[i_][1]])
            cosT, cosB = T([128, 1024], F32, "cos")
            sinT, sinB = T([128, 1024], F32, "sin")
            P.dma("sp", cosT, K["k_cos"], dst=cosB)
            P.dma("sp", sinT, K["k_sin"], dst=sinB)
        for i_ in range(2):
            P.op("pool", mset(sets[i_]["v"][0].rearrange("p a k n -> p (a k n)"), 1.0), writes=[sets[i_]["v"][1]])
        for hf in range(2):
            P.dma("sp", gqk[hf * 64:(hf + 1) * 64, 0:1], I["gq"][i].rearrange("(d o) -> d o", o=1), dst=gqkB)
            P.dma("sp", gqk[hf * 64:(hf + 1) * 64, 1:2], I["gk"][i].rearrange("(d o) -> d o", o=1), dst=gqkB)
        bgs = {"g": None}
        if g == 0 and do_ffn and l + 1 < n_layers:
            wms = [T([128, 8, 256], BF16, f"wm{i_}") for i_ in range(4)]
            set_rot([4, 5, 6])
            bgs["g"] = adaln_steps(l + 1, wms, ps[7], psB[7])

        def bg_step(n):
            if bgs["g"] is not None:
                for _ in range(n):
                    next(bgs["g"], None)

        win = I["w_ain"][i].rearrange("(k p) n -> p k n", p=128)
        wout = I["w_aout"][i]

        def headnorm(p_, pb, gcol, c0, rope, out_bf, outB, out_f32=None, out_f32B=None):
            P.op("act", act(hsq, p_, AF.Square), reads=[pb], writes=[hsqB])
            p2, p2B = rot()
            P.op("pe", mm_one(p2, bones, hsq, True, True), reads=[bonesB, hsqB], writes=[p2B])
            rstd_from(p2, p2B, 1.0 / 64, hr, hrB, hr, hrB)
            if out_f32 is not None:
                P.op("dve", stt(out_f32, p_, gqk[:, gcol:gcol + 1], hr, ALU.mult, ALU.mult), reads=[pb, gqkB, hrB], writes=[out_f32B])
                P.op("act", act(out_bf, out_f32, AF.Copy), reads=[out_f32B], writes=[outB])
                return
            if not rope:
                P.op("dve", stt(out_bf, p_, gqk[:, gcol:gcol + 1], hr, ALU.mult, ALU.mult), reads=[pb, gqkB, hrB], writes=[outB])
                return
            P.op("dve", stt(qnb, p_, gqk[:, gcol:gcol + 1], hr, ALU.mult, ALU.mult), reads=[pb, gqkB, hrB], writes=[qnbB])
            p3, p3B = rot()
            P.op("pe", mm_one(p3, rperm, qnb, True, True), reads=[rpermB, qnbB], writes=[p3B])
            P.op("pool", tt(rt1, qnb, cosT[:, c0:c0 + 512], ALU.mult), reads=[qnbB, cosB], writes=[rt1B])
            P.op("dve", tt(rt2, p3, sinT[:, c0:c0 + 512], ALU.mult), reads=[p3B, sinB], writes=[rt2B])
            P.op("pool", tt(out_bf, rt1, rt2, ALU.add), reads=[rt1B, rt2B], writes=[outB])

        def unit_proj(u, par):
            qT_, qTB = sets[par]["q"]
            kT_, kTB = sets[par]["k"]
            vp, vpB = sets[par]["v"]
            isA = u < 4
            j = u % 4
            qcol = (0 if isA else 1024) + 128 * j
            w_, wB = nxt("wst", wst)
            P.dma("pool", w_[:, :, 0:128], win[:, :, qcol:qcol + 128], dst=wB)
            if isA:
                kc0 = 512 + 64 * j
                vc0 = 768 + 64 * j
                P.dma("pool", w_[:, :, 128:192], win[:, :, kc0:kc0 + 64], dst=wB)
                P.dma("pool", w_[:, :, 192:256], win[:, :, kc0:kc0 + 64], dst=wB)
                P.dma("pool", w_[:, :, 256:320], win[:, :, vc0:vc0 + 64], dst=wB)
                nv = 64
                mrow = 128 * j
            else:
                kc0 = 1536 + 128 * j
                vc0 = 2048 + 128 * j
                P.dma("pool", w_[:, :, 128:256], win[:, :, kc0:kc0 + 128], dst=wB)
                P.dma("pool", w_[:, :, 256:384], win[:, :, vc0:vc0 + 128], dst=wB)
                nv = 128
                mrow = 512 + 128 * j
            if g == 1:
                ck = I["cak"][i] if isA else I["cbk"][i]
                cv = I["cav"][i] if isA else I["cbv"][i]
                if isA:
                    ksrc = ck[:, 64 * j:64 * j + 64].rearrange("(t p) c -> p t c", p=128)
                    P.dma("sp", kcs[:, :, 0:64], ksrc, dst=kcsB)
                    P.dma("sp", kcs[:, :, 64:128], ksrc, dst=kcsB)
                    vsrc = cv[:, 64 * j:64 * j + 64].rearrange("(t p) c -> p t c", p=128)
                    P.dma("pool", vp[:, 0, 8:12, 0:64], vsrc, dst=vpB)
                    P.dma("pool", vp[:, 1, 8:12, 64:128], vsrc, dst=vpB)
                else:
                    ksrc = ck[:, 128 * j:128 * j + 128].rearrange("(t p) c -> p t c", p=128)
                    P.dma("sp", kcs, ksrc, dst=kcsB)
                    P.dma("pool", vp[:, 0, 8:12, 0:64], cv[:, 128 * j:128 * j + 64].rearrange("(t p) c -> p t c", p=128), dst=vpB)
                    P.dma("pool", vp[:, 1, 8:12, 64:128], cv[:, 128 * j + 64:128 * j + 128].rearrange("(t p) c -> p t c", p=128), dst=vpB)
                p_, pb = rot()
                P.op("pe", tr_many([(p_[:, t * 128:(t + 1) * 128], kcs[:, t, :], ident) for t in range(4)]),
                     reads=[kcsB, identB], writes=[pb])
                P.op("act", act(kT_[:, 1024:1536], p_, AF.Copy), reads=[pb], writes=[kTB])
            for t in range(2):
                c0 = g * 1024 + t * 512
                hb = hB[g][t]
                p_, pb = rot()
                P.op("pe", mm_group(p_, [(w_[:, k, 0:128], hT[:, k, c0:c0 + 512]) for k in range(8)]), reads=[wB, hb], writes=[pb])
                if isA:
                    headnorm(p_, pb, 0, t * 512, g == 1, qT_[:, t * 512:(t + 1) * 512], qTB)
                else:
                    P.op("act", act(qT_[:, t * 512:(t + 1) * 512], p_, AF.Copy), reads=[pb], writes=[qTB])
                p_, pb = rot()
                P.op("pe", mm_group(p_, [(w_[:, k, 128:256], hT[:, k, c0:c0 + 512]) for k in range(8)]), reads=[wB, hb], writes=[pb])
                kdst = kT_[:, t * 512:(t + 1) * 512]
                if g == 0:
                    if KDBG < 2:
                        P.op("act", act(kdst, p_, AF.Copy), reads=[pb], writes=[kTB])
                        continue
                    if isA:
                        headnorm(p_, pb, 1, t * 512, False, kdst, kTB, kf32, kf32B)
                    else:
                        P.op("dve", cp(kf32, p_), reads=[pb], writes=[kf32B])
                        P.op("act", act(kdst, p_, AF.Copy), reads=[pb], writes=[kTB])
                    nkc = 64 if isA else 128
                    p4, p4B = rot()
                    P.op("pe", tr_many([(p4[:, q4 * 128:q4 * 128 + nkc], kf32[0:nkc, q4 * 128:(q4 + 1) * 128], ident[0:nkc, 0:nkc])
                                        for q4 in range(4)]), reads=[kf32B, identB], writes=[p4B])
                    P.op("dve", cp(ostg[:, t * 4:(t + 1) * 4, 0:nkc], p4.rearrange("p (a b) -> p a b", a=4)[:, :, 0:nkc]),
                         reads=[p4B], writes=[ostgB])
                else:
                    if isA:
                        headnorm(p_, pb, 1, t * 512, True, kdst, kTB)
                    else:
                        P.op("act", act(kdst, p_, AF.Copy), reads=[pb], writes=[kTB])
            bg_step(2)
            if g == 0 and KDBG >= 2:
                nkc = 64 if isA else 128
                okey = "nak" if isA else "nbk"
                for tk_ in range(8):
                    dsto = O[okey][tk_ // 2, i, (tk_ % 2) * 128:(tk_ % 2 + 1) * 128, nkc * j:nkc * (j + 1)]
                    P.dma("sp", dsto, ostg[:, tk_, 0:nkc], src=ostgB)
            for tq in (range(2) if KDBG >= 3 else []):
                p_, pb = rot()
                for t4 in range(4):
                    tk = tq * 4 + t4
                    c0 = g * 1024 + tk * 128
                    P.op("pe", mm_group(p_[:, t4 * 128:t4 * 128 + nv], [(hT[:, k, c0:c0 + 128], w_[:, k, 256:256 + nv]) for k in range(8)]),
                         reads=[wB, hB[g][tk // 4]], writes=[pb])
                pv = p_.rearrange("p (a b) -> p a b", a=4)
                ts4 = slice(tq * 4, tq * 4 + 4)
                if isA:
                    P.op("act", act(vp[:, 0, ts4, 0:64], pv[:, :, 0:64], AF.Copy), reads=[pb], writes=[vpB])
                    P.op("dve", cp(vp[:, 1, ts4, 64:128], pv[:, :, 0:64]), reads=[pb], writes=[vpB])
                else:
                    P.op("act", act(vp[:, 0, ts4, 0:64], pv[:, :, 0:64], AF.Copy), reads=[pb], writes=[vpB])
                    P.op("dve", cp(vp[:, 1, ts4, 64:128], pv[:, :, 64:128]), reads=[pb], writes=[vpB])
                if g == 0:
                    P.op("act", act(vst2[:, ts4, 0:nv], pv[:, :, 0:nv], AF.Copy), reads=[pb], writes=[vst2B])
            if g == 0 and KDBG >= 3:
                okey = "nav" if isA else "nbv"
                for tk_ in range(8):
                    dsto = O[okey][tk_ // 2, i, (tk_ % 2) * 128:(tk_ % 2 + 1) * 128, nv * j:nv * (j + 1)]
                    P.dma("sp", dsto, vst2[:, tk_, 0:nv], src=vst2B)

            bg_step(1)
            return dict(isA=isA, j=j, mrow=mrow, par=par)

        def unit_core(cx):
            isA, j, par = cx["isA"], cx["j"], cx["par"]
            wo, woB = nxt("wo", wo_)
            P.dma("pool", wo, wout[cx["mrow"]:cx["mrow"] + 128, :], dst=woB)
            cx["wo"], cx["woB"] = wo, woB
            onT_, onTB = sets[par]["on"]
            qT_, qTB = sets[par]["q"]
            kT_, kTB = sets[par]["k"]
            vp, vpB = sets[par]["v"]

            def run_batch(jobs, xc0, ntok):
                entries = [(ji, e, kt) for ji, jb in enumerate(jobs) for kt in jb["ktiles"] for e in range(2)]
                nent = len(entries)
                first = {}
                last = {}
                for ix, (ji, e, kt) in enumerate(entries):
                    first.setdefault((ji, e), ix)
                    last[(ji, e)] = ix
                stiles = {}

                def emit_s(ix):
                    ji, e, kt = entries[ix]
                    jb = jobs[ji]
                    nq, qc0 = jb["nq"], jb["qc0"]
                    r0 = 64 * e
                    p_, pb = rot()
                    bias = jb.get("bias")
                    if bias is not None and kt in bias["wt"]:
                        bt, btB = bias["tab"][e]
                        wt = bias["wt"][kt]
                        P.op("pe", mm_group(p_[:, 0:nq], [(kT_[r0:r0 + 64, kt * 128:(kt + 1) * 128], qT_[r0:r0 + 64, qc0:qc0 + nq]),
                                                         (identbf, bt[:, wt, 0:nq])]),
                             reads=[kTB, qTB, identbfB, btB], writes=[pb])
                    else:
                        P.op("pe", mm_one(p_[:, 0:nq], kT_[r0:r0 + 64, kt * 128:(kt + 1) * 128], qT_[r0:r0 + 64, qc0:qc0 + nq], True, True),
                             reads=[kTB, qTB], writes=[pb])
                    pp, ppB = nxt("pbuf", pbuf)
                    P.op("act", act(pp[:, 0:nq], p_[:, 0:nq], AF.Exp, scale=0.125), reads=[pb], writes=[ppB])
                    stiles[ix] = (pp, ppB)

                def emit_pv(ix):
                    ji, e, kt = entries[ix]
                    nq = jobs[ji]["nq"]
                    pp, ppB = stiles.pop(ix)
                    acc, accB = ps[2 * ji + e], psB[2 * ji + e]
                    P.op("pe", mm_one(acc[:, 0:nq], vp[:, e, kt, :], pp[:, 0:nq], ix == first[(ji, e)], ix == last[(ji, e)]),
                         reads=[vpB, ppB], writes=[accB])
                    if ix == last[(ji, 1)]:
                        oc = jobs[ji]["qc0"]
                        X, XB = ps[2 * ji], psB[2 * ji]
                        Y, YB = ps[2 * ji + 1], psB[2 * ji + 1]
                        P.op("act", act(rden[0:64, 0:nq], X[64:128, 0:nq], AF.Ln), reads=[XB], writes=[rdenB])
                        P.op("act", act(rden[64:128, 0:nq], Y[0:64, 0:nq], AF.Ln), reads=[YB], writes=[rdenB])
                        P.op("act", act(rden[:, 0:nq], rden[:, 0:nq], AF.Exp, scale=-1.0), reads=[rdenB], writes=[rdenB])
                        P.op("dve", tt(onT_[0:64, oc:oc + nq], X[0:64, 0:nq], rden[0:64, 0:nq], ALU.mult), reads=[XB, rdenB], writes=[onTB])
                        P.op("dve", tt(onT_[64:128, oc:oc + nq], Y[64:128, 0:nq], rden[64:128, 0:nq], ALU.mult), reads=[YB, rdenB], writes=[onTB])

                DEPTH_ = 3
                for ix in range(min(DEPTH_, nent)):
                    emit_s(ix)
                for ix in range(nent):
                    emit_pv(ix)
                    if ix + DEPTH_ < nent:
                        emit_s(ix + DEPTH_)

            if KDBG < 4:
                pass
            elif g == 0:
                for bq in range(2):
                    run_batch([dict(qc0=sq * 256, nq=256, ktiles=[2 * sq, 2 * sq + 1]) for sq in (2 * bq, 2 * bq + 1)], bq * 512, 512)
                    bg_step(1)
            else:
                if isA:
                    run_batch([dict(qc0=qt * 512, nq=512, ktiles=list(range(12))) for qt in range(2)], 0, 1024)
                else:
                    jobs = []
                    for qt in range(2):
                        wtiles = list(range(0, 6)) if qt == 0 else list(range(2, 8))
                        tabs = []
                        for e in range(2):
                            h = 2 * j + e
                            bt, btB = btab[qt * 2 + e]
                            for wi, kt in enumerate(wtiles):
                                for par in range(2):
                                    krow = 2 * kt + par
                                    qrows = [qr for qr in range(8 * qt, 8 * qt + 8)
                                             if min(max(qr - 4, 0), 8) <= krow <= min(max(qr - 4, 0), 8) + 7]
                                    if not qrows:
                                        continue
                                    q0, q1 = qrows[0], qrows[-1]
                                    nqr = q1 - q0 + 1
                                    a0_ = q0 - krow + 7
                                    srcap = tabscr[i, h * 15 + a0_:h * 15 + a0_ + nqr, :].rearrange("a (k q) -> k a q", k=64)
                                    dstap = bt[par * 64:(par + 1) * 64, wi, (q0 - 8 * qt) * 64:(q1 + 1 - 8 * qt) * 64].rearrange(
                                        "k (a q) -> k a q", q=64)
                                    P.dma("sp", dstap, srcap, dst=btB, src=tabB[i])
                            tabs.append((bt, btB))
                        jobs.append(dict(qc0=qt * 512, nq=512, ktiles=[8, 9, 10, 11] + wtiles,
                                         bias={"wt": {kt: wi for wi, kt in enumerate(wtiles)}, "tab": tabs}))
                    run_batch(jobs, 0, 1024)
        ulist = list(range(8) if units is None else units)
        cx = unit_proj(ulist[0], 0)
        done = []
        for n_, u in enumerate(ulist):
            cx_next = unit_proj(ulist[n_ + 1], (n_ + 1) % 2) if n_ + 1 < len(ulist) else None
            unit_core(cx)
            done.append(cx)
            if len(done) == 2 or cx_next is None:
                for t5 in range(2):
                    oc = t5 * 512
                    for d in range(8):
                        p_, pb = rot()
                        P.op("pe", mm_group(p_, [(c_["wo"][:, d * 128:(d + 1) * 128], sets[c_["par"]]["on"][0][:, oc:oc + 512]) for c_ in done]),
                             reads=[c_["woB"] for c_ in done] + [sets[c_["par"]]["on"][1] for c_ in done], writes=[pb])
                        x_update(p_, pb, d, g * 1024 + oc, 512, s, g, xB[g][t5])
                done = []
            cx = cx_next
        if bgs["g"] is not None:
            for _ in bgs["g"]:
                pass
        P.phase_end()

    def gla_group(l, g):
        jl = l // 2
        s = 1
        P.phase_begin()
        st["p"] = PH_BASE
        set_rot([2, 3, 4, 5, 6, 7])
        norm_mod(s, g)
        P.barrier()
        st["p"] = NORM_BASE
        qT_, qTB = T([128, 1024], BF16, "gq")
        kT_, kTB = T([128, 1024], BF16, "gk")
        ktm, ktmB = T([128, 8, 128], BF16, "ktm")
        vtm, vtmB = T([128, 8, 256], BF16, "vtm")
        sgtm, sgtmB = T([128, 8, 256], BF16, "sgtm")
        osv, osvB = T([128, 8, 256], F32, "osv")
        S_ = alloc([128, 2, 256], F32, "S")
        SB = [Buf("S0"), Buf("S1")]
        Sbf = alloc([128, 2, 256], BF16, "Sbf")
        SbfB = [Buf("Sbf0"), Buf("Sbf1")]
        dT = [T([16, 1024], BF16, f"dT{i_}") for i_ in range(2)]
        wgup, wgupB = T([16, 2, 128], BF16, "wgup")
        bgup, bgupB = T([1, 2, 128], BF16, "bgup")
        ones1, ones1B = T([1, 128], BF16, "ones1")
        gnrow, gnrowB = T([1, 256], F32, "gnrow")
        gnb, gnbB = T([128, 256], F32, "gnb")
        oT, oTB = T([128, 2, 1024], BF16, "oT")
        wo, woB = T([128, 2, 1024], BF16, "gwo")
        prep = []
        for d_ in range(2):
            pr = {}
            pr["qin"] = T([128, 8, 128], BF16, f"qin{d_}")
            pr["kin"] = T([128, 8, 128], BF16, f"kin{d_}")
            pr["kout"] = T([128, 8, 128], BF16, f"kout{d_}")
            pr["atm"] = T([128, 8, 128], BF16, f"atm{d_}")
            pr["dec"] = T([128, 8], F32, f"dec{d_}")
            prep.append(pr)
        spt, sptB = T([128, 4, 128], F32, "spt")
        ext, extB = T([128, 512], F32, "ext")
        E1, E1B = T([128, 4, 128], F32, "E1")
        E2, E2B = T([128, 4, 128], F32, "E2")
        E3, E3B = T([128, 4, 128], F32, "E3")
        otmps = []
        for i_ in range(3):
            otmps.append({"n1": T([128, 256], F32, f"on1{i_}"), "n2": T([128, 256], F32, f"on2{i_}")})
        osqs = [T([128, 256], F32, f"osqs{i_}") for i_ in range(3)]
        ss8, ss8B = T([128, 8], F32, "ss8")
        rs8, rs8B = T([128, 8], F32, "rs8")
        if l == 1:
            print(f"[sbuf] gla g={g} used up to {st['p']} of {SB_HI} (free {SB_HI - st['p']})")
        P.dma("pool", ones1, K["k_ones"][0:1, :], dst=ones1B)
        P.dma("sp", gnrow, I["g_gn"][jl].rearrange("(o n) -> o n", o=1), dst=gnrowB)
        p_, pb = rot()
        P.op("pe", mm_one(p_[:, 0:256], onesf[0:1, :], gnrow[0:1, :], True, True), reads=[onesfB, gnrowB], writes=[pb])
        P.op("dve", cp(gnb, p_[:, 0:256]), reads=[pb], writes=[gnbB])
        win = I["w_gin"][jl].rearrange("(k p) n -> p k n", p=128)
        w_, wB = nxt("wst", wst)
        P.dma("pool", w_[:, :, 0:32], win[:, :, 3072:3104], dst=wB)
        for t in range(2):
            c0 = g * 1024 + t * 512
            for d_ in range(2):
                p_, pb = rot()
                P.op("pe", mm_group(p_[0:16, :], [(w_[:, k, d_ * 16:(d_ + 1) * 16], hT[:, k, c0:c0 + 512]) for k in range(8)]),
                     reads=[wB, hB[g][t]], writes=[pb])
                P.op("act", act(dT[d_][0][:, t * 512:(t + 1) * 512], p_[0:16, :], AF.Copy), reads=[pb], writes=[dT[d_][1]])

        def qk_chunks(h):
            wa, waB = nxt("wst", wst)
            P.dma("pool", wa[:, :, 0:128], win[:, :, 128 * h:128 * h + 128], dst=waB)
            P.dma("pool", wa[:, :, 128:256], win[:, :, 512 + 128 * h:512 + 128 * h + 128], dst=waB)

            def cq(t):
                c0 = g * 1024 + t * 512
                p_, pb = rot()
                P.op("pe", mm_group(p_, [(wa[:, k, 0:128], hT[:, k, c0:c0 + 512]) for k in range(8)]), reads=[waB, hB[g][t]], writes=[pb])
                P.op("act", act(qT_[:, t * 512:(t + 1) * 512], p_, AF.Copy), reads=[pb], writes=[qTB])

            def ck(t):
                c0 = g * 1024 + t * 512
                p_, pb = rot()
                P.op("pe", mm_group(p_, [(wa[:, k, 128:256], hT[:, k, c0:c0 + 512]) for k in range(8)]), reads=[waB, hB[g][t]], writes=[pb])
                P.op("dve", cp(kT_[:, t * 512:(t + 1) * 512], p_), reads=[pb], writes=[kTB])

            def ckt(t):
                c0 = g * 1024 + t * 512
                p_, pb = rot()
                for t4 in range(4):
                    cc = c0 + t4 * 128
                    P.op("pe", mm_group(p_[:, t4 * 128:(t4 + 1) * 128], [(hT[:, k, cc:cc + 128], wa[:, k, 128:256]) for k in range(8)]),
                         reads=[waB, hB[g][t]], writes=[pb])
                P.op("act", act(ktm[:, t * 4:(t + 1) * 4, :], p_.rearrange("p (a b) -> p a b", a=4), AF.Copy), reads=[pb], writes=[ktmB])

            return [lambda: cq(0), lambda: ck(0), lambda: ckt(0), lambda: cq(1), lambda: ck(1), lambda: ckt(1)]

        hlist = list(range(4) if units is None else units)
        pending_qk = qk_chunks(hlist[0])
        for hi_, h in enumerate(hlist):
            for c_ in pending_qk:
                c_()
            pending_qk = []
            wb_, wbB = nxt("wst", wst)
            P.dma("pool", wb_[:, :, 0:256], win[:, :, 1024 + 256 * h:1024 + 256 * h + 256], dst=wbB)
            P.dma("pool", wb_[:, :, 256:512], win[:, :, 2048 + 256 * h:2048 + 256 * h + 256], dst=wbB)
            P.dma("pool", wo, I["w_gout"][jl][256 * h:256 * h + 256, :].rearrange("(c p) n -> p c n", p=128), dst=woB)
            P.dma("pool", wgup, I["w_gup"][jl][:, :, 128 * h:128 * h + 128].rearrange("d r n -> r d n"), dst=wgupB)
            P.dma("pool", bgup, I["b_gup"][jl][:, 128 * h:128 * h + 128].rearrange("(o d) n -> o d n", o=1), dst=bgupB)
            def proj_v(tk2):
                pv_, pvB = rot()
                for t2 in range(2):
                    tk = tk2 * 2 + t2
                    cc = g * 1024 + tk * 128
                    P.op("pe", mm_group(pv_[:, t2 * 256:(t2 + 1) * 256], [(hT[:, k, cc:cc + 128], wb_[:, k, 0:256]) for k in range(8)]),
                         reads=[wbB, hB[g][tk // 4]], writes=[pvB])
                P.op("dve", cp(vtm[:, tk2 * 2:tk2 * 2 + 2, :], pv_.rearrange("p (a b) -> p a b", a=2)), reads=[pvB], writes=[vtmB])

            def proj_g(tk2):
                pg_, pgB = rot()
                for t2 in range(2):
                    tk = tk2 * 2 + t2
                    cc = g * 1024 + tk * 128
                    P.op("pe", mm_group(pg_[:, t2 * 256:(t2 + 1) * 256], [(hT[:, k, cc:cc + 128], wb_[:, k, 256:512]) for k in range(8)]),
                         reads=[wbB, hB[g][tk // 4]], writes=[pgB])
                P.op("act", act(sgtm[:, tk2 * 2:tk2 * 2 + 2, :], pg_.rearrange("p (a b) -> p a b", a=2), AF.Silu), reads=[pgB], writes=[sgtmB])

            qv = qT_.rearrange("p (a b) -> p a b", b=128)
            kv = kT_.rearrange("p (a b) -> p a b", b=128)
            bi = 0
            for dr in range(2):
                pr = prep[dr]
                qin, qinB = pr["qin"]
                kin, kinB = pr["kin"]
                kout, koutB = pr["kout"]
                atm, atmB = pr["atm"]
                dec, decB = pr["dec"]
                for hh in range(2):
                    tsl = slice(4 * hh, 4 * hh + 4)
                    p1, p1B = rot()
                    for t4 in range(4):
                        cc = (4 * hh + t4) * 128
                        P.op("pe", mm_group(p1[:, t4 * 128:(t4 + 1) * 128], [(dT[dr][0][0:16, cc:cc + 128], wgup[0:16, dr, :]),
                                                                             (ones1[0:1, :], bgup[0:1, dr, :])]),
                             reads=[dT[dr][1], wgupB, ones1B, bgupB], writes=[p1B])
                    P.op("act", act(ext, p1, AF.Exp, scale=-1.0), reads=[p1B], writes=[extB])
                    P.op("act", act(spt.rearrange("p a b -> p (a b)"), ext, AF.Ln, bias=c_one, scale=1.0), reads=[extB, cstB], writes=[sptB])
                    proj_v(bi)
                    p2, p2B = rot()
                    p3, p3B = rot()
                    for t4 in range(4):
                        P.op("pe", mm_one(p2[:, t4 * 128:(t4 + 1) * 128], spt[:, t4, :], tri[:, 2 + dr, :], True, True), reads=[sptB, triB], writes=[p2B])
                    for t4 in range(4):
                        P.op("pe", mm_one(p3[:, t4 * 128:(t4 + 1) * 128], tri[:, 4 + dr, :], spt[:, t4, :], True, True), reads=[sptB, triB], writes=[p3B])
                    P.op("act", act(E1.rearrange("p a b -> p (a b)"), p2, AF.Exp, bias=c_lnq, scale=1.0), reads=[p2B, cstB], writes=[E1B])
                    P.op("act", act(E2.rearrange("p a b -> p (a b)"), p2, AF.Exp, scale=-1.0), reads=[p2B], writes=[E2B])
                    dcols = p2[:, 127:512:128] if dr == 0 else p2[:, 0:512:128]
                    P.op("act", act(dec[:, 4 * hh:4 * hh + 4], dcols, AF.Exp), reads=[p2B], writes=[decB])
                    P.op("act", act(E3.rearrange("p a b -> p (a b)"), p3, AF.Exp), reads=[p3B], writes=[E3B])
                    P.op("dve", tt(qin[:, tsl, :], qv[:, tsl, :], E1, ALU.mult), reads=[qTB, E1B], writes=[qinB])
                    P.op("pool", tt(kin[:, tsl, :], kv[:, tsl, :], E2, ALU.mult), reads=[kTB, E2B], writes=[kinB])
                    P.op("dve", tt(kout[:, tsl, :], ktm[:, tsl, :], E3, ALU.mult), reads=[ktmB, E3B], writes=[koutB])
                    proj_g(bi)
                    bi += 1
                    p4, p4B = rot()
                    for t4 in range(4):
                        tk = 4 * hh + t4
                        P.op("pe", mm_one(p4[:, t4 * 128:(t4 + 1) * 128], kin[:, tk, :], qin[:, tk, :], True, True),
                             reads=[kinB, qinB], writes=[p4B])
                    for t4 in range(4):
                        tk = 4 * hh + t4
                        P.op("dve", tt(atm[:, tk, :], p4[:, t4 * 128:(t4 + 1) * 128], tri[:, dr, :], ALU.mult), reads=[p4B, triB], writes=[atmB])

            def stage(dr, tk, first_visit):
                pr = prep[dr]
                qin, qinB = pr["qin"]
                kout, koutB = pr["kout"]
                atm, atmB = pr["atm"]
                dec, decB = pr["dec"]
                cc = tk * 128
                oacc, oaccB = ps[dr], psB[dr]
                P.op("pe", mm_one(oacc[:, 0:256], atm[:, tk, :], vtm[:, tk, :], True, False), reads=[atmB, vtmB], writes=[oaccB])
                P.op("pe", mm_one(oacc[:, 0:256], qin[:, tk, :], Sbf[:, dr, :], False, True), reads=[qinB, SbfB[dr]], writes=[oaccB])
                pu, puB = rot()
                P.op("pe", mm_one(pu[:, 0:256], kout[:, tk, :], vtm[:, tk, :], True, True), reads=[koutB, vtmB], writes=[puB])
                dcol = dec[:, tk:tk + 1]
                P.op("dve", stt(Sbf[:, dr, :], S_[:, dr, :], dcol, pu[:, 0:256], ALU.mult, ALU.add),
                     reads=[SB[dr], decB, puB], writes=[SbfB[dr]])
                P.op("dve", stt(S_[:, dr, :], S_[:, dr, :], dcol, pu[:, 0:256], ALU.mult, ALU.add),
                     reads=[SB[dr], decB, puB], writes=[SB[dr]])
                if first_visit:
                    P.op("act", act(osv[:, tk, :], oacc[:, 0:256], AF.Copy), reads=[oaccB], writes=[osvB])
                else:
                    P.op("dve", tt(osv[:, tk, :], oacc[:, 0:256], osv[:, tk, :], ALU.add), reads=[oaccB, osvB], writes=[osvB])

            def out_stats():
                for tk in range(8):
                    sq_, sqB_ = osqs[tk % 3]
                    P.op("act", (lambda e, o=sq_, i_=osv[:, tk, :], a_=ss8[:, tk:tk + 1]: e.activation(out=o, in_=i_, func=AF.Square, accum_out=a_)),
                         reads=[osvB], writes=[sqB_, ss8B], strict=True)
                rstd_from(ss8, ss8B, 1.0 / 256, rs8, rs8B, rs8, rs8B)

            def out_stage(tk):
                ot = nxt("otmp", otmps)
                on1, on1B = ot["n1"]
                on2, on2B = ot["n2"]
                P.op("dve", stt(on1, osv[:, tk, :], rs8[:, tk:tk + 1], gnb, ALU.mult, ALU.mult), reads=[osvB, rs8B, gnbB], writes=[on1B])
                P.op("pool", tt(on2, on1, sgtm[:, tk, :], ALU.mult), reads=[on1B, sgtmB], writes=[on2B])
                return on2, on2B

            def out_transpose(tk, on2, on2B):
                cc = tk * 128
                p5, p5B = rot()
                P.op("pe", tr_many([(p5[:, c2 * 128:(c2 + 1) * 128], on2[:, c2 * 128:(c2 + 1) * 128], ident) for c2 in range(2)]),
                     reads=[on2B, identB], writes=[p5B])
                P.op("dve", cp(oT[:, :, cc:cc + 128], p5[:, 0:256].rearrange("p (a b) -> p a b", a=2)), reads=[p5B], writes=[oTB])

            seqs = [(sq * 2, 2) for sq in range(4)] if g == 0 else [(0, 8)]
            for si, (t0, n) in enumerate(seqs):
                for dr in range(2):
                    if g == 0:
                        P.op("pool", mset(S_[:, dr, :], 0.0), writes=[SB[dr]])
                        P.op("pool", mset(Sbf[:, dr, :], 0.0), writes=[SbfB[dr]])
                    else:
                        P.dma("sp", S_[:, dr, :], I["sgla"][jl, dr, h], dst=SB[dr])
                        P.op("act", act(Sbf[:, dr, :], S_[:, dr, :], AF.Copy), reads=[SB[dr]], writes=[SbfB[dr]])
                for step in range(n):
                    stage(0, t0 + step, step < n // 2)
                    stage(1, t0 + n - 1 - step, step < n // 2)
                if g == 0:
                    for dr in range(2):
                        P.dma("sp", O["ngla"][si, jl, dr, h], S_[:, dr, :], src=SB[dr])
            nxt_chunks = qk_chunks(hlist[hi_ + 1]) if hi_ + 1 < len(hlist) else []
            out_stats()
            pend = None
            for tk in range(8):
                cur = (tk,) + out_stage(tk)
                if nxt_chunks:
                    nxt_chunks.pop(0)()
                if pend is not None:
                    out_transpose(*pend)
                pend = cur
            out_transpose(*pend)
            for c_ in nxt_chunks:
                c_()
            for t in range(2):
                for d in range(8):
                    p_, pb = rot()
                    P.op("pe", mm_group(p_, [(wo[:, c2, d * 128:(d + 1) * 128], oT[:, c2, t * 512:(t + 1) * 512]) for c2 in range(2)]),
                         reads=[woB, oTB], writes=[pb])
                    x_update(p_, pb, d, g * 1024 + t * 512, 512, s, g, xB[g][t])
        P.phase_end()

    for l in range(n_layers):
        CUR["l"] = l
        if l == 0 or not do_ffn:
            set_rot(range(1, 8))
            adaln_now(l)
        if do_ffn:
            ffn(l, 0, 0)
        if do_mix:
            P.barrier()
            if l % 2 == 0:
                set_rot([4, 5, 6, 7])
                if 1 in groups:
                    attn_build_tab(l // 2)
                for g_ in groups:
                    attn_group(l, g_)
            else:
                for g_ in groups:
                    gla_group(l, g_)
        if do_ffn:
            ffn(l, 1, 2, bg_layer=(l + 1 if (l + 1 < n_layers and not (do_mix and l % 2 == 0 and 0 in groups)) else None),
                bg_cgs=range(36), bg_fin=(0, 1, 2), bg_rate=2)

    P.phase_begin()
    st["p"] = PH_BASE
    set_rot(range(8))
    if dbg:
        for g in range(2):
            for t in range(2):
                c0 = g * 1024 + t * 512
                P.dma("sp", O["dbg"][:, :, c0:c0 + 512], xT[:, :, c0:c0 + 512], src=xB[g][t])
    gfrow, gfrowB = T([1, 1024], F32, "gfrow")
    gfb, gfbB = T([128, 1024], F32, "gfb")
    P.dma("sp", gfrow, I["g_fin"].rearrange("(o n) -> o n", o=1), dst=gfrowB)
    for hf in range(2):
        p_, pb = rot()
        P.op("pe", mm_one(p_, onesf[0:1, :], gfrow[0:1, hf * 512:(hf + 1) * 512], True, True), reads=[onesfB, gfrowB], writes=[pb])
        P.op("dve", cp(gfb[:, hf * 512:(hf + 1) * 512], p_), reads=[pb], writes=[gfbB])
    ytm = [T([128, 1024], F32, f"ytm{i}") for i in range(2)]
    ysq, ysqB = T([128, 1024], F32, "ysq")
    yss, yssB = T([128, 1], F32, "yss")
    yrl, yrlB = T([128, 1], F32, "yrl")
    yrs, yrsB = T([128, 1], F32, "yrs")
    yo = [T([128, 1024], F32, f"yo{i}") for i in range(2)]
    for tt_ in range(16):
        g, tl = tt_ // 8, tt_ % 8
        c0 = g * 1024 + tl * 128
        yt, ytB = nxt("ytm", ytm)
        for half in range(2):
            p_, pb = rot()
            P.op("pe", tr_many([(p_[:, j * 128:(j + 1) * 128], xT[:, half * 4 + j, c0:c0 + 128], ident) for j in range(4)]),
                 reads=[xB[g][tl // 4], identB], writes=[pb])
            if half == 0:
                P.op("dve", cp(yt[:, 0:512], p_), reads=[pb], writes=[ytB])
            else:
                P.op("act", act(yt[:, 512:1024], p_, AF.Copy), reads=[pb], writes=[ytB])
        P.op("pool", tt(ysq, yt, yt, ALU.mult), reads=[ytB], writes=[ysqB])
        P.op("dve", (lambda e, o=yss, i_=ysq: e.reduce_sum(out=o, in_=i_, axis=AX.X)), reads=[ysqB], writes=[yssB])
        rstd_from(yss, yssB, 1.0 / D, yrs, yrsB, yrl, yrlB)
        yo_, yoB = nxt("yo", yo)
        P.op("dve", stt(yo_, yt, yrs[:, 0:1], gfb, ALU.mult, ALU.mult), reads=[ytB, yrsB, gfbB], writes=[yoB])
        dsto = (O["yp"] if g == 0 else O["ys"])[tl * 128:(tl + 1) * 128, :]
        P.dma("sp", dsto, yo_, src=yoB)

    import contextlib
    with contextlib.ExitStack() as es:
        sems = [es.enter_context(nc.semaphore(f"s{i}")) for i in range(P.nsem)]
        block = es.enter_context(nc.Block())

        @block.tensor
        def _(e):
            P.replay("pe", e, sems)

        @block.scalar
        def _(e):
            P.replay("act", e, sems)

        @block.vector
        def _(e):
            P.replay("dve", e, sems)

        @block.gpsimd
        def _(e):
            P.replay("pool", e, sems)

        @block.sync
        def _(e):
            P.replay("sp", e, sems)
            for sm, tot in P.semtot.items():
                if tot > 0:
                    e.wait_ge(sems[sm], tot)
            for en in ("pe", "act", "dve", "pool"):
                if P.cnt[en] > 0:
                    e.wait_ge(sems[P.semid(en)], P.cnt[en])
    return nc, P


def make_in_map(inp, core, consts):
    c = core
    f = lambda a: np.ascontiguousarray(np.asarray(a, dtype=np.float32))
    m = {
        "xp": f(inp["x_prompt"][4 * c:4 * c + 4]).reshape(1024, 1024),
        "xs": f(inp["x_sample"][c]).reshape(1024, 1024),
        "cak": f(inp["cache_a_k"][c]).reshape(2, 512, 256),
        "cav": f(inp["cache_a_v"][c]).reshape(2, 512, 256),
        "cbk": f(inp["cache_b_k"][c]).reshape(2, 512, 512),
        "cbv": f(inp["cache_b_v"][c]).reshape(2, 512, 512),
        "sgla": f(inp["state_gla"][c]),
        "cond": f(np.stack([np.asarray(inp["c_ctx"]), np.asarray(inp["c"])[c]])),
        "w_mod": f(inp["w_mod"]), "b_mod": f(inp["b_mod"]), "g_norm": f(inp["g_norm"]),
        "wg": f(inp["w_ffn_gate"]), "wu": f(inp["w_ffn_up"]), "wd": f(inp["w_ffn_down"]),
        "w_ain": f(inp["w_attn_in"]), "w_aout": f(inp["w_attn_out"]), "gq": f(inp["g_qnorm"]), "gk": f(inp["g_knorm"]),
        "relb": f(inp["na_rel_bias"]),
        "w_gin": f(inp["w_gla_in"]), "w_gup": f(inp["w_gla_gup"]), "b_gup": f(inp["b_gla_gup"]), "g_gn": f(inp["g_gla_norm"]),
        "w_gout": f(inp["w_gla_out"]), "g_fin": f(inp["g_final"]),
    }
    m.update(consts)
    return m


_CACHE = {}


def kernel(**inputs):
    if "nc" not in _CACHE:
        _CACHE["nc"] = build_program()[0]
        _CACHE["consts"] = _constants()
    nc = _CACHE["nc"]
    consts = _CACHE["consts"]
    in_maps = [make_in_map(inputs, c, consts) for c in range(N_CORES)]
    res = run_bass_kernel_spmd(nc, in_maps, core_ids=list(range(N_CORES)))
    R = res.results
    y_prompt = np.concatenate([R[c]["yp"].reshape(4, 256, 1024) for c in range(N_CORES)], axis=0)
    y_sample = np.stack([R[c]["ys"].reshape(1024, 1024) for c in range(N_CORES)], axis=0)
    nak = np.concatenate([R[c]["nak"].reshape(4, 2, 256, 4, 64) for c in range(N_CORES)], axis=0)
    nav = np.concatenate([R[c]["nav"].reshape(4, 2, 256, 4, 64) for c in range(N_CORES)], axis=0)
    nbk = np.concatenate([R[c]["nbk"].reshape(4, 2, 256, 8, 64) for c in range(N_CORES)], axis=0)
    nbv = np.concatenate([R[c]["nbv"].reshape(4, 2, 256, 8, 64) for c in range(N_CORES)], axis=0)
    ngla = np.concatenate([R[c]["ngla"].reshape(4, 2, 2, 4, 128, 256) for c in range(N_CORES)], axis=0)
    return (y_prompt.astype(np.float32), y_sample.astype(np.float32), nak.astype(np.float32), nav.astype(np.float32),
            nbk.astype(np.float32), nbv.astype(np.float32), ngla.astype(np.float32))
```

```python
import math
import os
import numpy as np
KDBG = int(os.environ.get('KDBG', '99'))
import concourse.bass as bass
import concourse.mybir as mybir
from concourse.bass_utils import run_bass_kernel_spmd

F32 = mybir.dt.float32
BF16 = mybir.dt.bfloat16
AF = mybir.ActivationFunctionType
ALU = mybir.AluOpType
AX = mybir.AxisListType

D = 1024
NT = 2048
DFF = 2816
NF = DFF // 128
EPS = 1e-6
N_CORES = 8
DEPTH = 4
NEG = -30000.0


_PHASE = {"list": None}


class Buf:
    __slots__ = ("name", "w", "r", "sem", "excl")

    def __init__(self, name, excl=False):
        self.name = name
        self.excl = excl
        self.w = None
        self.r = []
        self.sem = {}
        if _PHASE["list"] is not None:
            _PHASE["list"].append(self)


class Prog:
    ENG = ("pe", "act", "dve", "pool", "sp")

    def __init__(self):
        self.prog = {e: [] for e in self.ENG}
        self.cnt = {e: 0 for e in self.ENG}
        self.seen = {e: {} for e in self.ENG}
        self.nsem = len(self.ENG)
        self.semtot = {}
        self.free = {"hw": [], "sw": []}
        self.pending = {e: [] for e in self.ENG}

    def semid(self, e):
        return self.ENG.index(e)

    def _need(self, eng, dep, out):
        sem, val, _ = dep
        if out.get(sem, 0) < val:
            out[sem] = val

    def _deps(self, eng, reads, writes, skip_sem=None, is_dma=False):
        out = {}
        for d in self.pending[eng]:
            self._need(eng, d, out)
        self.pending[eng] = []
        for b in reads:
            if b.w is not None:
                self._need(eng, b.w, out)
            if b.excl:
                for d in b.r:
                    if d[2] != eng:
                        self._need(eng, d, out)
        for b in writes:
            strict = is_dma or eng == "pool"
            if b.w is not None and (strict or b.w[2] != eng) and not (b.w[2] == "dma" and b.w[0] == skip_sem):
                self._need(eng, b.w, out)
            for d in b.r:
                if strict or d[2] != eng:
                    self._need(eng, d, out)
        res = []
        for sem, val in out.items():
            if self.seen[eng].get(sem, 0) >= val:
                continue
            self.seen[eng][sem] = val
            res.append((sem, val))
        return res

    def op(self, eng, fn, reads=(), writes=()):
        waits = self._deps(eng, reads, writes)
        self.cnt[eng] += 1
        idx = self.cnt[eng]
        sem = self.semid(eng)
        self.prog[eng].append((waits, fn, sem, 1))
        for b in reads:
            b.r.append((sem, idx, eng))
        for b in writes:
            b.w = (sem, idx, eng)
            b.r = []

    def dma(self, q, out, in_, dst=None, src=None):
        owner = dst if dst is not None else src
        cls = "sw" if q == "pool" else "hw"
        if cls not in owner.sem:
            if self.free[cls]:
                owner.sem[cls] = self.free[cls].pop()
            else:
                owner.sem[cls] = self.nsem
                self.nsem += 1
                self.semtot[owner.sem[cls]] = 0
        sm = owner.sem[cls]
        waits = self._deps(q, [src] if src is not None else [], [dst] if dst is not None else [], skip_sem=sm, is_dma=True)
        self.semtot[sm] += 16
        tot = self.semtot[sm]
        self.prog[q].append((waits, (lambda e, o=out, i=in_: e.dma_start(out=o, in_=i)), sm, 16))
        if src is not None:
            src.r.append((sm, tot, "dma"))
        if dst is not None:
            dst.r = []
            dst.w = (sm, tot, "dma")

    def barrier(self):
        deps = [(self.semid(e), self.cnt[e], "bar") for e in self.ENG if self.cnt[e] > 0]
        deps += [(sm, tot, "bar") for sm, tot in self.semtot.items() if tot > 0]
        for e in self.ENG:
            self.pending[e] = list(deps)

    def phase_begin(self):
        _PHASE["list"] = []

    def phase_end(self):
        self.barrier()
        for b in _PHASE["list"]:
            for cls, sm in b.sem.items():
                self.free[cls].append(sm)
            b.sem = {}
        _PHASE["list"] = None

    def replay(self, eng, e, sems):
        for waits, fn, sem, inc in self.prog[eng]:
            for (ws, wv) in waits:
                e.wait_ge(sems[ws], wv)
            ins = fn(e)
            ins.then_inc(sems[sem], inc)


def _constants():
    c = {}
    c["k_ident"] = np.eye(128, dtype=np.float32)
    c["k_ones"] = np.ones((128, 128), np.float32)
    bo = np.zeros((128, 128), np.float32)
    bo[:64, :64] = 1.0
    bo[64:, 64:] = 1.0
    c["k_bones"] = bo
    R = np.zeros((128, 128), np.float32)
    for m in range(128):
        blk = (m // 32) * 32
        j = m % 32
        if j < 16:
            R[blk + j + 16, m] = -1.0
        else:
            R[blk + j - 16, m] = 1.0
    c["k_rperm"] = R
    t = np.arange(1024)
    freqs = 10000.0 ** (-np.arange(0, 32, 2, dtype=np.float64) / 32.0)
    cosT = np.zeros((128, 1024), np.float64)
    sinT = np.zeros((128, 1024), np.float64)
    for p in range(128):
        d = p % 64
        pos = (t // 64) if d < 32 else (t % 64)
        f = freqs[(d % 32) % 16]
        ang = (pos.astype(np.float32) * np.float32(f)).astype(np.float64)
        cosT[p] = np.cos(ang)
        sinT[p] = np.sin(ang)
    c["k_cos"] = cosT.astype(np.float32)
    c["k_sin"] = sinT.astype(np.float32)
    s = np.arange(128)[:, None]
    tt = np.arange(128)[None, :]
    same = (s // 128) == (tt // 128)
    maskf = (same & (s <= tt)).astype(np.float32)
    maskb = (same & (s >= tt)).astype(np.float32)
    suf = (same & (s > tt)).astype(np.float32)
    sub = (same & (s < tt)).astype(np.float32)
    c["k_tri"] = np.stack([maskf, maskb, -maskf / 16.0, -maskb / 16.0, -suf / 16.0, -sub / 16.0]).astype(np.float32)
    op = np.zeros((128, 2, 128), np.float32)
    op[:, 0, :64] = 1.0
    op[:, 1, 64:] = 1.0
    c["k_opad"] = op.reshape(128, 256)
    kc = np.arange(64)[:, None]
    qc = np.arange(64)[None, :]
    dc = kc - qc + 15
    ws = np.clip(qc - 8, 0, 48)
    inwin = (kc >= ws) & (kc < ws + 16)
    oh = np.zeros((31, 64, 64), np.float32)
    for b in range(31):
        oh[b] = np.where((dc == b) & inwin, 8.0, 0.0)
    c["k_oh8"] = oh.reshape(31, 4096)
    c["k_cmask"] = np.where(inwin, 0.0, NEG).astype(np.float32).reshape(1, 4096)
    perm = np.zeros((120, 120), np.float32)
    for h in range(8):
        for a in range(15):
            perm[h * 15 + a, h * 15 + (14 - a)] = 1.0
    c["k_perm"] = perm
    return c


CONST_SHAPES = {
    "k_ident": (128, 128), "k_ones": (128, 128), "k_bones": (128, 128), "k_rperm": (128, 128),
    "k_cos": (128, 1024), "k_sin": (128, 1024), "k_tri": (6, 128, 128), "k_opad": (128, 256),
    "k_oh8": (31, 4096), "k_cmask": (1, 4096), "k_perm": (120, 120),
}

IN_SHAPES = {
    "xp": (1024, 1024), "xs": (1024, 1024),
    "cak": (2, 512, 256), "cav": (2, 512, 256), "cbk": (2, 512, 512), "cbv": (2, 512, 512),
    "sgla": (2, 2, 4, 128, 256), "cond": (2, 1024),
    "w_mod": (4, 1024, 9216), "b_mod": (4, 9216), "g_norm": (4, 3, 1024),
    "wg": (4, 2, 1024, DFF), "wu": (4, 2, 1024, DFF), "wd": (4, 2, DFF, 1024),
    "w_ain": (2, 1024, 2560), "w_aout": (2, 1024, 1024), "gq": (2, 64), "gk": (2, 64),
    "relb": (2, 8, 15, 31),
    "w_gin": (2, 1024, 3104), "w_gup": (2, 2, 16, 512), "b_gup": (2, 2, 512), "g_gn": (2, 256),
    "w_gout": (2, 1024, 1024), "g_fin": (1024,),
}
OUT_SHAPES = {
    "yp": (1024, 1024), "ys": (1024, 1024),
    "nak": (4, 2, 256, 256), "nav": (4, 2, 256, 256), "nbk": (4, 2, 256, 512), "nbv": (4, 2, 256, 512),
    "ngla": (4, 2, 2, 4, 128, 256),
}


def build_program(n_layers=DEPTH, do_ffn=True, do_mix=True, dbg=False, groups=(0, 1), units=None):
    nc = bass.Bass("TRN2", target_bir_lowering=False)
    P = Prog()
    I = {k: nc.dram_tensor(k, list(v), F32, kind="ExternalInput").ap() for k, v in IN_SHAPES.items()}
    K = {k: nc.dram_tensor(k, list(v), F32, kind="ExternalInput").ap() for k, v in CONST_SHAPES.items()}
    O = {k: nc.dram_tensor(k, list(v), F32, kind="ExternalOutput").ap() for k, v in OUT_SHAPES.items()}
    if dbg:
        O["dbg"] = nc.dram_tensor("dbg", [128, 8, NT], F32, kind="ExternalOutput").ap()
    tabscr = nc.dram_tensor("tabscr", [2, 120, 4096], BF16, kind="Internal").ap()
    tabB = [Buf("tabscr0"), Buf("tabscr1")]

    SB_LO = 16512
    SB_HI = int(nc.sbuf_top) if not callable(nc.sbuf_top) else int(nc.sbuf_top())
    st = {"p": SB_LO, "n": 0}

    def alloc(shape, dt, name="t"):
        nb = 1
        for s_ in shape[1:]:
            nb *= s_
        nb *= 2 if dt == BF16 else 4
        nb = (nb + 31) // 32 * 32
        off = st["p"]
        st["p"] += nb
        assert st["p"] <= SB_HI, f"SBUF overflow at {name}: {st['p']} > {SB_HI}"
        st["n"] += 1
        return nc.alloc_sbuf_tensor_at(f"{name}_{st['n']}", list(shape), dt, offset=off).ap()

    def T(shape, dt, name="t"):
        return alloc(shape, dt, name), Buf(name)

    xT = alloc([128, 8, NT], F32, "xT")
    xB = [[Buf(f"x{g}{t}") for t in range(2)] for g in range(2)]
    hT = alloc([128, 8, NT], BF16, "hT")
    hB = [[Buf(f"h{g}{t}") for t in range(2)] for g in range(2)]
    ident, identB = T([128, 128], F32, "ident")
    identbf, identbfB = T([128, 128], BF16, "identbf")
    onesbf, onesbfB = T([128, 128], BF16, "onesbf")
    bones, bonesB = T([128, 128], BF16, "bones")
    rperm, rpermB = T([128, 128], BF16, "rperm")
    tri, triB = T([128, 6, 128], F32, "tri")
    opad, opadB = T([128, 2, 128], BF16, "opad")
    onesf, onesfB = T([1, 128], F32, "onesf")
    cst, cstB = T([128, 4], F32, "cst")
    MODS = []
    for i_ in range(2):
        MODS.append({"mod": T([128, 72, 2], F32, f"mod{i_}"), "acoef": T([128, 3, 8, 2], F32, f"acoef{i_}"),
                     "gcoef": T([128, 3, 8, 2], F32, f"gcoef{i_}")})
    CUR = {"l": 0}
    gnt, gntB = T([128, 24], F32, "gnt")
    bmt, bmtB = T([128, 72], F32, "bmt")
    scT, scB = T([128, 2, 8], BF16, "scT")
    condt, condB = T([128, 16], F32, "condt")
    vstg, vstgB = T([72, 128], F32, "vstg")
    wst = [T([128, 8, 512], BF16, f"wst{i}") for i in range(3)]
    NORM_BASE = st["p"]
    sqb, sqbB = T([128, 8, 512], BF16, "sqb")
    rstd, rstdB = T([128, 512], F32, "rstd")
    lnt, lntB = T([128, 512], F32, "lnt")
    ntmp = [T([128, 512], F32, f"ntmp{i}") for i in range(3)]
    PH_BASE = st["p"]

    ps = [nc.alloc_psum_tensor(f"ps{i}", [128, 512], F32).ap() for i in range(8)]
    psB = [Buf(f"ps{i}", excl=True) for i in range(8)]
    rot_state = {"banks": list(range(8)), "i": 0}

    def rot():
        b = rot_state["banks"][rot_state["i"] % len(rot_state["banks"])]
        rot_state["i"] += 1
        return ps[b], psB[b]

    def set_rot(banks):
        rot_state["banks"] = list(banks)
        rot_state["i"] = 0

    cyc = {}

    def nxt(key, lst):
        i = cyc.get(key, 0)
        cyc[key] = i + 1
        return lst[i % len(lst)]

    def mm_group(out, pairs):
        def f(e):
            n = len(pairs)
            ins = None
            for i, (l, r) in enumerate(pairs):
                ins = e.matmul(out, lhsT=l, rhs=r, start=(i == 0), stop=(i == n - 1))
            return ins
        return f

    def mm_one(out, l, r, start, stop):
        return lambda e: e.matmul(out, lhsT=l, rhs=r, start=start, stop=stop)

    def tr_one(out, in_, idn):
        return lambda e: e.transpose(out, in_, idn)

    def tr_many(items):
        def f(e):
            ins = None
            for (o, i_, idn) in items:
                ins = e.transpose(o, i_, idn)
            return ins
        return f

    def act(out, in_, func, bias=None, scale=None):
        kw = {}
        if bias is not None:
            kw["bias"] = bias
        if scale is not None:
            kw["scale"] = scale
        return lambda e: e.activation(out=out, in_=in_, func=func, **kw)

    def tt(out, a, b, op):
        return lambda e: e.tensor_tensor(out=out, in0=a, in1=b, op=op)

    def stt(out, in0, scalar, in1, op0, op1):
        return lambda e: e.scalar_tensor_tensor(out=out, in0=in0, scalar=scalar, in1=in1, op0=op0, op1=op1)

    def tsm(out, in0, s1):
        return lambda e: e.tensor_scalar_mul(out=out, in0=in0, scalar1=s1)

    def cp(out, in_):
        return lambda e: e.tensor_copy(out=out, in_=in_)

    def mset(ap, v):
        return lambda e: e.memset(ap, v)

    def recip(out, in_):
        return lambda e: e.reciprocal(out=out, in_=in_)

    P.dma("sp", ident, K["k_ident"], dst=identB)
    P.dma("pool", identbf, K["k_ident"], dst=identbfB)
    P.dma("pool", onesbf, K["k_ones"], dst=onesbfB)
    P.dma("pool", bones, K["k_bones"], dst=bonesB)
    P.dma("pool", rperm, K["k_rperm"], dst=rpermB)
    P.dma("sp", tri, K["k_tri"].rearrange("a p n -> p a n"), dst=triB)
    P.dma("pool", opad, K["k_opad"].rearrange("p (a n) -> p a n", a=2), dst=opadB)
    P.dma("sp", onesf, K["k_ones"][0:1, :], dst=onesfB)
    P.op("dve", mset(cst[:, 0:1], EPS), writes=[cstB])
    P.op("dve", mset(cst[:, 1:2], 1.0), writes=[cstB])
    P.op("dve", mset(cst[:, 2:3], math.log(128.0 ** -0.5)), writes=[cstB])
    P.op("dve", mset(cst[:, 3:4], 0.0), writes=[cstB])
    c_eps, c_one, c_lnq = cst[:, 0:1], cst[:, 1:2], cst[:, 2:3]

    def load_fm(src2d, n, dst, dstB):
        P.dma("sp", vstg[0:n, :], src2d, dst=vstgB)
        p_, pb = rot()
        P.op("pe", tr_one(p_[:, 0:n], vstg[0:n, :], ident[0:n, 0:n]), reads=[vstgB, identB], writes=[pb])
        P.op("dve", cp(dst, p_[:, 0:n]), reads=[pb], writes=[dstB])

    P.phase_begin()
    st["p"] = PH_BASE
    xin = [T([128, 1024], F32, f"xin{i}") for i in range(3)]
    for tt_ in range(16):
        g, tl = tt_ // 8, tt_ % 8
        src = (I["xp"] if g == 0 else I["xs"])[tl * 128:(tl + 1) * 128, :]
        xi, xiB = nxt("xin", xin)
        P.dma("sp", xi, src, dst=xiB)
        c0 = g * 1024 + tl * 128
        for half in range(2):
            p_, pb = rot()
            P.op("pe", tr_many([(p_[:, j * 128:(j + 1) * 128], xi[:, (half * 4 + j) * 128:(half * 4 + j + 1) * 128], ident)
                                for j in range(4)]), reads=[xiB, identB], writes=[pb])
            eng = "dve" if half == 0 else "act"
            dst = xT[:, half * 4:half * 4 + 4, c0:c0 + 128]
            srcp = p_.rearrange("p (a b) -> p a b", a=4)
            if eng == "dve":
                P.op("dve", cp(dst, srcp), reads=[pb], writes=[xB[g][tl // 4]])
            else:
                P.op("act", act(dst, srcp, AF.Copy), reads=[pb], writes=[xB[g][tl // 4]])

    load_fm(I["cond"].rearrange("j (c p) -> (j c) p", p=128), 16, condt, condB)
    P.op("act", act(scT.rearrange("p j k -> p (j k)"), condt, AF.Silu), reads=[condB], writes=[scB])

    def adaln_steps(l, slots, pm, pmB, cgs=range(36), fin=(0, 1, 2)):
        modt, modB = MODS[l % 2]["mod"]
        acoef, acoefB = MODS[l % 2]["acoef"]
        gcoef, gcoefB = MODS[l % 2]["gcoef"]
        load_fm(I["b_mod"][l].rearrange("(c p) -> c p", p=128), 72, bmt, bmtB)
        load_fm(I["g_norm"][l].rearrange("s (c p) -> (s c) p", p=128), 24, gnt, gntB)
        wsrc = I["w_mod"][l].rearrange("(k p) n -> p k n", p=128)
        pending = []
        for cg in cgs:
            w_, wB = slots[cg % len(slots)]
            P.dma("pool", w_[:, :, 0:256], wsrc[:, :, cg * 256:(cg + 1) * 256], dst=wB)
            pending.append((cg, w_, wB))
            if len(pending) < len(slots):
                continue
            yield
            cg0, w0, w0B = pending.pop(0)
            for j in range(2):
                ch = cg0 * 2 + j
                P.op("pe", mm_group(pm[:, ch * 2:ch * 2 + 2],
                                    [(w0[:, k, j * 128:(j + 1) * 128], scT[:, :, k]) for k in range(8)]),
                     reads=[w0B, scB], writes=[pmB])
        for (cg0, w0, w0B) in pending:
            for j in range(2):
                ch = cg0 * 2 + j
                P.op("pe", mm_group(pm[:, ch * 2:ch * 2 + 2],
                                    [(w0[:, k, j * 128:(j + 1) * 128], scT[:, :, k]) for k in range(8)]),
                     reads=[w0B, scB], writes=[pmB])
        pmv = pm[:, 0:144].rearrange("p (c j) -> p c j", j=2)
        for s in fin:
            for j in range(2):
                P.op("dve", tt(modt[:, 24 * s:24 * s + 24, j], pmv[:, 24 * s:24 * s + 24, j], bmt[:, 24 * s:24 * s + 24], ALU.add),
                     reads=[pmB, bmtB], writes=[modB])
        for s in fin:
            for j in range(2):
                P.op("dve", stt(acoef[:, s, :, j], modt[:, (3 * s + 1) * 8:(3 * s + 2) * 8, j], 1.0,
                                gnt[:, s * 8:(s + 1) * 8], ALU.add, ALU.mult), reads=[modB, gntB], writes=[acoefB])
                P.op("dve", tsm(gcoef[:, s, :, j], modt[:, (3 * s + 2) * 8:(3 * s + 3) * 8, j],
                                0.5 if s != 1 else 1.0), reads=[modB], writes=[gcoefB])
        yield

    def adaln_now(l, cgs=range(36), fin=(0, 1, 2)):
        for _ in adaln_steps(l, wst, ps[0], psB[0], cgs, fin):
            pass

    def rstd_from(ss_ap, ssB, scale, out_ap, outB, tmp_ap, tmpB):
        P.op("act", act(tmp_ap, ss_ap, AF.Ln, bias=c_eps, scale=scale), reads=[ssB, cstB], writes=[tmpB])
        P.op("act", act(out_ap, tmp_ap, AF.Exp, scale=-0.5), reads=[tmpB], writes=[outB])

    def norm_mod(s, g):
        for t in range(2):
            c0 = g * 1024 + t * 512
            xb, hb = xB[g][t], hB[g][t]
            P.op("act", act(sqb, xT[:, :, c0:c0 + 512], AF.Square), reads=[xb], writes=[sqbB])
            p_, pb = rot()
            P.op("pe", mm_group(p_, [(onesbf, sqb[:, k, :]) for k in range(8)]), reads=[sqbB, onesbfB], writes=[pb])
            rstd_from(p_, pb, 1.0 / D, rstd, rstdB, lnt, lntB)
            for k in range(8):
                tm, tmB = nxt("ntmp", ntmp)
                P.op("dve", tt(tm, xT[:, k, c0:c0 + 512], rstd, ALU.mult), reads=[xb, rstdB], writes=[tmB])
                modt, modB = MODS[CUR["l"] % 2]["mod"]
                acoef, acoefB = MODS[CUR["l"] % 2]["acoef"]
                P.op("act", act(hT[:, k, c0:c0 + 512], tm, AF.Identity, bias=modt[:, 3 * s * 8 + k, g:g + 1],
                                scale=acoef[:, s, k, g:g + 1]), reads=[tmB, modB, acoefB], writes=[hb])

    def x_update(p_ap, pb, d, c0, n, s, g, xb, eng="dve"):
        xa = xT[:, d, c0:c0 + n]
        gcoef, gcoefB = MODS[CUR["l"] % 2]["gcoef"]
        P.op(eng, stt(xa, p_ap, gcoef[:, s, d, g:g + 1], xa, ALU.mult, ALU.add), reads=[pb, gcoefB, xb], writes=[xb])

    def ffn(l, idx, s, bg_layer=None, bg_cgs=None, bg_fin=None, bg_rate=2):
        P.phase_begin()
        st["p"] = PH_BASE
        parts = [(0, 5), (5, 10), (10, 14), (14, 18), (18, 22)]
        actT = alloc([128, 5, NT], BF16, "act")
        actB = [Buf(f"act{t}") for t in range(4)]
        wdt = [T([128, 5, 1024], BF16, f"wd{i}") for i in range(2)]
        sgt = [T([128, 512], BF16, f"sg{i}") for i in range(3)]
        bg = None
        if bg_layer is not None:
            wms = [T([128, 8, 256], BF16, f"wm{i}") for i in range(4)]
            set_rot(range(1, 8))
            bg = adaln_steps(bg_layer, wms, ps[0], psB[0], bg_cgs, bg_fin)
        else:
            set_rot(range(8))
        norm_mod(s, 0)
        norm_mod(s, 1)
        wgs = I["wg"][l, idx].rearrange("(k p) n -> p k n", p=128)
        wus = I["wu"][l, idx].rearrange("(k p) n -> p k n", p=128)
        wds = I["wd"][l, idx].rearrange("(f p) n -> p f n", p=128)
        for (f0, f1) in parts:
            nf = f1 - f0
            wd_, wdB = nxt("wd", wdt)
            P.dma("pool", wd_[:, 0:nf, :], wds[:, f0:f1, :], dst=wdB)
            fs = f0
            while fs < f1:
                nsub = min(2, f1 - fs)
                w_, wB = nxt("wst", wst)
                P.dma("pool", w_[:, :, 0:nsub * 128], wgs[:, :, fs * 128:(fs + nsub) * 128], dst=wB)
                P.dma("pool", w_[:, :, 256:256 + nsub * 128], wus[:, :, fs * 128:(fs + nsub) * 128], dst=wB)
                for fi in range(nsub):
                    fl = fs + fi - f0
                    for t in range(4):
                        c0 = t * 512
                        hb = hB[t // 2][t % 2]
                        pg, pgB = rot()
                        P.op("pe", mm_group(pg, [(w_[:, k, fi * 128:(fi + 1) * 128], hT[:, k, c0:c0 + 512]) for k in range(8)]),
                             reads=[wB, hb], writes=[pgB])
                        pu, puB = rot()
                        P.op("pe", mm_group(pu, [(w_[:, k, 256 + fi * 128:256 + (fi + 1) * 128], hT[:, k, c0:c0 + 512]) for k in range(8)]),
                             reads=[wB, hb], writes=[puB])
                        sg, sgB = nxt("sg", sgt)
                        P.op("act", act(sg, pg, AF.Silu), reads=[pgB], writes=[sgB])
                        P.op("dve", tt(actT[:, fl, c0:c0 + 512], sg, pu, ALU.mult), reads=[sgB, puB], writes=[actB[t]])
                    if bg is not None:
                        for _ in range(bg_rate):
                            next(bg, None)
                fs += nsub
            for t in range(4):
                c0 = t * 512
                g = t // 2
                for d in range(8):
                    p_, pb = rot()
                    P.op("pe", mm_group(p_, [(wd_[:, fl, d * 128:(d + 1) * 128], actT[:, fl, c0:c0 + 512]) for fl in range(nf)]),
                         reads=[wdB, actB[t]], writes=[pb])
                    x_update(p_, pb, d, c0, 512, s, g, xB[g][t % 2])
        if bg is not None:
            for _ in bg:
                pass
        P.phase_end()

    def attn_build_tab(i):
        P.phase_begin()
        st["p"] = PH_BASE
        relt, relB = T([120, 31], F32, "relt")
        relbf, relbfB = T([120, 31], BF16, "relbf")
        perm, permB = T([120, 120], BF16, "perm")
        btr, btrB = T([31, 120], BF16, "btr")
        oh8, oh8B = T([31, 4096], BF16, "oh8")
        cmask, cmaskB = T([1, 4096], BF16, "cmask")
        ones1, ones1B = T([1, 128], BF16, "ones1")
        tabsb, tabsbB = T([120, 4096], BF16, "tabsb")
        P.dma("sp", relt, I["relb"][i].rearrange("h a b -> (h a) b"), dst=relB)
        P.dma("pool", perm, K["k_perm"], dst=permB)
        P.dma("pool", oh8, K["k_oh8"], dst=oh8B)
        P.dma("pool", cmask, K["k_cmask"], dst=cmaskB)
        P.dma("pool", ones1, K["k_ones"][0:1, :], dst=ones1B)
        P.op("dve", cp(relbf, relt), reads=[relB], writes=[relbfB])
        p_, pb = rot()
        P.op("pe", mm_one(p_[0:31, 0:120], relbf, perm, True, True), reads=[relbfB, permB], writes=[pb])
        P.op("dve", cp(btr, p_[0:31, 0:120]), reads=[pb], writes=[btrB])
        for cgi in range(8):
            p_, pb = rot()
            cs = slice(cgi * 512, (cgi + 1) * 512)
            P.op("pe", mm_group(p_[0:120, :], [(btr, oh8[:, cs]), (ones1[0:1, 0:120], cmask[0:1, cs])]),
                 reads=[btrB, oh8B, ones1B, cmaskB], writes=[pb])
            P.op("act" if cgi % 2 else "dve", (act(tabsb[:, cs], p_[0:120, :], AF.Copy) if cgi % 2 else cp(tabsb[:, cs], p_[0:120, :])),
                 reads=[pb], writes=[tabsbB])
        P.dma("sp", tabscr[i], tabsb, dst=tabB[i], src=tabsbB)
        P.phase_end()

    def attn_group(l, g):
        i = l // 2
        s = 1
        P.phase_begin()
        st["p"] = PH_BASE
        set_rot([4, 5, 6, 7])
        norm_mod(s, g)
        P.barrier()
        st["p"] = NORM_BASE
        NK = 12 if g == 1 else 8
        sets = []
        for i_ in range(2):
            sets.append({"q": T([128, 1024], BF16, f"qT{i_}"), "k": T([128, NK * 128], BF16, f"kT{i_}"),
                         "v": T([128, 2, NK, 128], BF16, f"vp{i_}"), "on": T([128, 1024], BF16, f"onT{i_}")})
        pbuf = [T([128, 512], BF16, f"pb{i_}") for i_ in range(4)]
        rden, rdenB = T([128, 512], F32, "rden")
        wo_ = [T([128, 1024], BF16, f"wo{i_}") for i_ in range(2)]
        gqk, gqkB = T([128, 2], F32, "gqk")
        hsq, hsqB = T([128, 512], BF16, "hsq")
        hr, hrB = T([128, 512], F32, "hr")
        if g == 0:
            xtmp = [T([128, 512], F32, f"xtmp{i_}") for i_ in range(2)]
            kf32, kf32B = T([128, 512], F32, "kf32")
            ostg, ostgB = T([128, 8, 128], F32, "ostg")
            vst2, vst2B = T([128, 8, 128], F32, "vst2")
        else:
            qnb, qnbB = T([128, 512], BF16, "qnb")
            rt1, rt1B = T([128, 512], F32, "rt1")
            rt2, rt2B = T([128, 512], F32, "rt2")
            kcs, kcsB = T([128, 4, 128], F32, "kcs")
            btab = [T([128, 6, 512], BF16, f"btab{i_}") for i_ in range(4)]
            for i_ in range(4):
                P.op("pool", mset(btab[i_][0].rearrange("p a b -> p (a b)"), NEG), writes=[btab[i_][1]])
            cosT, cosB = T([128, 1024], F32, "cos")
            sinT, sinB = T([128, 1024], F32, "sin")
            P.dma("sp", cosT, K["k_cos"], dst=cosB)
            P.dma("sp", sinT, K["k_sin"], dst=sinB)
        for i_ in range(2):
            P.op("pool", mset(sets[i_]["v"][0].rearrange("p a k n -> p (a k n)"), 1.0), writes=[sets[i_]["v"][1]])
        for hf in range(2):
            P.dma("sp", gqk[hf * 64:(hf + 1) * 64, 0:1], I["gq"][i].rearrange("(d o) -> d o", o=1), dst=gqkB)
            P.dma("sp", gqk[hf * 64:(hf + 1) * 64, 1:2], I["gk"][i].rearrange("(d o) -> d o", o=1), dst=gqkB)
        bgs = {"g": None}
        if g == 0 and do_ffn and l + 1 < n_layers:
            wms = [T([128, 8, 256], BF16, f"wm{i_}") for i_ in range(4)]
            set_rot([4, 5, 6])
            bgs["g"] = adaln_steps(l + 1, wms, ps[7], psB[7])

        def bg_step(n):
            if bgs["g"] is not None:
                for _ in range(n):
                    next(bgs["g"], None)

        win = I["w_ain"][i].rearrange("(k p) n -> p k n", p=128)
        wout = I["w_aout"][i]

        def headnorm(p_, pb, gcol, c0, rope, out_bf, outB, out_f32=None, out_f32B=None):
            P.op("act", act(hsq, p_, AF.Square), reads=[pb], writes=[hsqB])
            p2, p2B = rot()
            P.op("pe", mm_one(p2, bones, hsq, True, True), reads=[bonesB, hsqB], writes=[p2B])
            rstd_from(p2, p2B, 1.0 / 64, hr, hrB, hr, hrB)
            if out_f32 is not None:
                P.op("dve", stt(out_f32, p_, gqk[:, gcol:gcol + 1], hr, ALU.mult, ALU.mult), reads=[pb, gqkB, hrB], writes=[out_f32B])
                P.op("act", act(out_bf, out_f32, AF.Copy), reads=[out_f32B], writes=[outB])
                return
            if not rope:
                P.op("dve", stt(out_bf, p_, gqk[:, gcol:gcol + 1], hr, ALU.mult, ALU.mult), reads=[pb, gqkB, hrB], writes=[outB])
                return
            P.op("dve", stt(qnb, p_, gqk[:, gcol:gcol + 1], hr, ALU.mult, ALU.mult), reads=[pb, gqkB, hrB], writes=[qnbB])
            p3, p3B = rot()
            P.op("pe", mm_one(p3, rperm, qnb, True, True), reads=[rpermB, qnbB], writes=[p3B])
            P.op("pool", tt(rt1, qnb, cosT[:, c0:c0 + 512], ALU.mult), reads=[qnbB, cosB], writes=[rt1B])
            P.op("dve", tt(rt2, p3, sinT[:, c0:c0 + 512], ALU.mult), reads=[p3B, sinB], writes=[rt2B])
            P.op("pool", tt(out_bf, rt1, rt2, ALU.add), reads=[rt1B, rt2B], writes=[outB])

        def unit_proj(u, par):
            qT_, qTB = sets[par]["q"]
            kT_, kTB = sets[par]["k"]
            vp, vpB = sets[par]["v"]
            isA = u < 4
            j = u % 4
            qcol = (0 if isA else 1024) + 128 * j
            w_, wB = nxt("wst", wst)
            P.dma("pool", w_[:, :, 0:128], win[:, :, qcol:qcol + 128], dst=wB)
            if isA:
                kc0 = 512 + 64 * j
                vc0 = 768 + 64 * j
                P.dma("pool", w_[:, :, 128:192], win[:, :, kc0:kc0 + 64], dst=wB)
                P.dma("pool", w_[:, :, 192:256], win[:, :, kc0:kc0 + 64], dst=wB)
                P.dma("pool", w_[:, :, 256:320], win[:, :, vc0:vc0 + 64], dst=wB)
                nv = 64
                mrow = 128 * j
            else:
                kc0 = 1536 + 128 * j
                vc0 = 2048 + 128 * j
                P.dma("pool", w_[:, :, 128:256], win[:, :, kc0:kc0 + 128], dst=wB)
                P.dma("pool", w_[:, :, 256:384], win[:, :, vc0:vc0 + 128], dst=wB)
                nv = 128
                mrow = 512 + 128 * j
            if g == 1:
                ck = I["cak"][i] if isA else I["cbk"][i]
                cv = I["cav"][i] if isA else I["cbv"][i]
                if isA:
                    ksrc = ck[:, 64 * j:64 * j + 64].rearrange("(t p) c -> p t c", p=128)
                    P.dma("sp", kcs[:, :, 0:64], ksrc, dst=kcsB)
                    P.dma("sp", kcs[:, :, 64:128], ksrc, dst=kcsB)
                    vsrc = cv[:, 64 * j:64 * j + 64].rearrange("(t p) c -> p t c", p=128)
                    P.dma("pool", vp[:, 0, 8:12, 0:64], vsrc, dst=vpB)
                    P.dma("pool", vp[:, 1, 8:12, 64:128], vsrc, dst=vpB)
                else:
                    ksrc = ck[:, 128 * j:128 * j + 128].rearrange("(t p) c -> p t c", p=128)
                    P.dma("sp", kcs, ksrc, dst=kcsB)
                    P.dma("pool", vp[:, 0, 8:12, 0:64], cv[:, 128 * j:128 * j + 64].rearrange("(t p) c -> p t c", p=128), dst=vpB)
                    P.dma("pool", vp[:, 1, 8:12, 64:128], cv[:, 128 * j + 64:128 * j + 128].rearrange("(t p) c -> p t c", p=128), dst=vpB)
                p_, pb = rot()
                P.op("pe", tr_many([(p_[:, t * 128:(t + 1) * 128], kcs[:, t, :], ident) for t in range(4)]),
                     reads=[kcsB, identB], writes=[pb])
                P.op("act", act(kT_[:, 1024:1536], p_, AF.Copy), reads=[pb], writes=[kTB])
            for t in range(2):
                c0 = g * 1024 + t * 512
                hb = hB[g][t]
                p_, pb = rot()
                P.op("pe", mm_group(p_, [(w_[:, k, 0:128], hT[:, k, c0:c0 + 512]) for k in range(8)]), reads=[wB, hb], writes=[pb])
                if isA:
                    headnorm(p_, pb, 0, t * 512, g == 1, qT_[:, t * 512:(t + 1) * 512], qTB)
                else:
                    P.op("act", act(qT_[:, t * 512:(t + 1) * 512], p_, AF.Copy), reads=[pb], writes=[qTB])
                p_, pb = rot()
                P.op("pe", mm_group(p_, [(w_[:, k, 128:256], hT[:, k, c0:c0 + 512]) for k in range(8)]), reads=[wB, hb], writes=[pb])
                kdst = kT_[:, t * 512:(t + 1) * 512]
                if g == 0:
                    if KDBG < 2:
                        P.op("act", act(kdst, p_, AF.Copy), reads=[pb], writes=[kTB])
                        continue
                    if isA:
                        headnorm(p_, pb, 1, t * 512, False, kdst, kTB, kf32, kf32B)
                    else:
                        P.op("dve", cp(kf32, p_), reads=[pb], writes=[kf32B])
                        P.op("act", act(kdst, p_, AF.Copy), reads=[pb], writes=[kTB])
                    nkc = 64 if isA else 128
                    p4, p4B = rot()
                    P.op("pe", tr_many([(p4[:, q4 * 128:q4 * 128 + nkc], kf32[0:nkc, q4 * 128:(q4 + 1) * 128], ident[0:nkc, 0:nkc])
                                        for q4 in range(4)]), reads=[kf32B, identB], writes=[p4B])
                    P.op("dve", cp(ostg[:, t * 4:(t + 1) * 4, 0:nkc], p4.rearrange("p (a b) -> p a b", a=4)[:, :, 0:nkc]),
                         reads=[p4B], writes=[ostgB])
                else:
                    if isA:
                        headnorm(p_, pb, 1, t * 512, True, kdst, kTB)
                    else:
                        P.op("act", act(kdst, p_, AF.Copy), reads=[pb], writes=[kTB])
            bg_step(2)
            if g == 0 and KDBG >= 2:
                nkc = 64 if isA else 128
                okey = "nak" if isA else "nbk"
                for tk_ in range(8):
                    dsto = O[okey][tk_ // 2, i, (tk_ % 2) * 128:(tk_ % 2 + 1) * 128, nkc * j:nkc * (j + 1)]
                    P.dma("sp", dsto, ostg[:, tk_, 0:nkc], src=ostgB)
            for tq in (range(2) if KDBG >= 3 else []):
                p_, pb = rot()
                for t4 in range(4):
                    tk = tq * 4 + t4
                    c0 = g * 1024 + tk * 128
                    P.op("pe", mm_group(p_[:, t4 * 128:t4 * 128 + nv], [(hT[:, k, c0:c0 + 128], w_[:, k, 256:256 + nv]) for k in range(8)]),
                         reads=[wB, hB[g][tk // 4]], writes=[pb])
                pv = p_.rearrange("p (a b) -> p a b", a=4)
                ts4 = slice(tq * 4, tq * 4 + 4)
                if isA:
                    P.op("act", act(vp[:, 0, ts4, 0:64], pv[:, :, 0:64], AF.Copy), reads=[pb], writes=[vpB])
                    P.op("dve", cp(vp[:, 1, ts4, 64:128], pv[:, :, 0:64]), reads=[pb], writes=[vpB])
                else:
                    P.op("act", act(vp[:, 0, ts4, 0:64], pv[:, :, 0:64], AF.Copy), reads=[pb], writes=[vpB])
                    P.op("dve", cp(vp[:, 1, ts4, 64:128], pv[:, :, 64:128]), reads=[pb], writes=[vpB])
                if g == 0:
                    P.op("act", act(vst2[:, ts4, 0:nv], pv[:, :, 0:nv], AF.Copy), reads=[pb], writes=[vst2B])
            if g == 0 and KDBG >= 3:
                okey = "nav" if isA else "nbv"
                for tk_ in range(8):
                    dsto = O[okey][tk_ // 2, i, (tk_ % 2) * 128:(tk_ % 2 + 1) * 128, nv * j:nv * (j + 1)]
                    P.dma("sp", dsto, vst2[:, tk_, 0:nv], src=vst2B)

            bg_step(1)
            return dict(isA=isA, j=j, mrow=mrow, par=par)

        def unit_core(cx):
            isA, j, par = cx["isA"], cx["j"], cx["par"]
            wo, woB = nxt("wo", wo_)
            P.dma("pool", wo, wout[cx["mrow"]:cx["mrow"] + 128, :], dst=woB)
            cx["wo"], cx["woB"] = wo, woB
            onT_, onTB = sets[par]["on"]
            qT_, qTB = sets[par]["q"]
            kT_, kTB = sets[par]["k"]
            vp, vpB = sets[par]["v"]

            def run_batch(jobs, xc0, ntok):
                entries = [(ji, e, kt) for ji, jb in enumerate(jobs) for kt in jb["ktiles"] for e in range(2)]
                nent = len(entries)
                first = {}
                last = {}
                for ix, (ji, e, kt) in enumerate(entries):
                    first.setdefault((ji, e), ix)
                    last[(ji, e)] = ix
                stiles = {}

                def emit_s(ix):
                    ji, e, kt = entries[ix]
                    jb = jobs[ji]
                    nq, qc0 = jb["nq"], jb["qc0"]
                    r0 = 64 * e
                    p_, pb = rot()
                    bias = jb.get("bias")
                    if bias is not None and kt in bias["wt"]:
                        bt, btB = bias["tab"][e]
                        wt = bias["wt"][kt]
                        P.op("pe", mm_group(p_[:, 0:nq], [(kT_[r0:r0 + 64, kt * 128:(kt + 1) * 128], qT_[r0:r0 + 64, qc0:qc0 + nq]),
                                                         (identbf, bt[:, wt, 0:nq])]),
                             reads=[kTB, qTB, identbfB, btB], writes=[pb])
                    else:
                        P.op("pe", mm_one(p_[:, 0:nq], kT_[r0:r0 + 64, kt * 128:(kt + 1) * 128], qT_[r0:r0 + 64, qc0:qc0 + nq], True, True),
                             reads=[kTB, qTB], writes=[pb])
                    pp, ppB = nxt("pbuf", pbuf)
                    P.op("act", act(pp[:, 0:nq], p_[:, 0:nq], AF.Exp, scale=0.125), reads=[pb], writes=[ppB])
                    stiles[ix] = (pp, ppB)

                def emit_pv(ix):
                    ji, e, kt = entries[ix]
                    nq = jobs[ji]["nq"]
                    pp, ppB = stiles.pop(ix)
                    acc, accB = ps[2 * ji + e], psB[2 * ji + e]
                    P.op("pe", mm_one(acc[:, 0:nq], vp[:, e, kt, :], pp[:, 0:nq], ix == first[(ji, e)], ix == last[(ji, e)]),
                         reads=[vpB, ppB], writes=[accB])
                    if ix == last[(ji, 1)]:
                        oc = jobs[ji]["qc0"]
                        X, XB = ps[2 * ji], psB[2 * ji]
                        Y, YB = ps[2 * ji + 1], psB[2 * ji + 1]
                        P.op("act", act(rden[0:64, 0:nq], X[64:128, 0:nq], AF.Ln), reads=[XB], writes=[rdenB])
                        P.op("act", act(rden[64:128, 0:nq], Y[0:64, 0:nq], AF.Ln), reads=[YB], writes=[rdenB])
                        P.op("act", act(rden[:, 0:nq], rden[:, 0:nq], AF.Exp, scale=-1.0), reads=[rdenB], writes=[rdenB])
                        P.op("dve", tt(onT_[0:64, oc:oc + nq], X[0:64, 0:nq], rden[0:64, 0:nq], ALU.mult), reads=[XB, rdenB], writes=[onTB])
                        P.op("dve", tt(onT_[64:128, oc:oc + nq], Y[64:128, 0:nq], rden[64:128, 0:nq], ALU.mult), reads=[YB, rdenB], writes=[onTB])

                DEPTH_ = 3
                for ix in range(min(DEPTH_, nent)):
                    emit_s(ix)
                for ix in range(nent):
                    emit_pv(ix)
                    if ix + DEPTH_ < nent:
                        emit_s(ix + DEPTH_)

            if KDBG < 4:
                pass
            elif g == 0:
                for bq in range(2):
                    run_batch([dict(qc0=sq * 256, nq=256, ktiles=[2 * sq, 2 * sq + 1]) for sq in (2 * bq, 2 * bq + 1)], bq * 512, 512)
                    bg_step(1)
            else:
                if isA:
                    run_batch([dict(qc0=qt * 512, nq=512, ktiles=list(range(12))) for qt in range(2)], 0, 1024)
                else:
                    jobs = []
                    for qt in range(2):
                        wtiles = list(range(0, 6)) if qt == 0 else list(range(2, 8))
                        tabs = []
                        for e in range(2):
                            h = 2 * j + e
                            bt, btB = btab[qt * 2 + e]
                            for wi, kt in enumerate(wtiles):
                                for par in range(2):
                                    krow = 2 * kt + par
                                    qrows = [qr for qr in range(8 * qt, 8 * qt + 8)
                                             if min(max(qr - 4, 0), 8) <= krow <= min(max(qr - 4, 0), 8) + 7]
                                    if not qrows:
                                        continue
                                    q0, q1 = qrows[0], qrows[-1]
                                    nqr = q1 - q0 + 1
                                    a0_ = q0 - krow + 7
                                    srcap = tabscr[i, h * 15 + a0_:h * 15 + a0_ + nqr, :].rearrange("a (k q) -> k a q", k=64)
                                    dstap = bt[par * 64:(par + 1) * 64, wi, (q0 - 8 * qt) * 64:(q1 + 1 - 8 * qt) * 64].rearrange(
                                        "k (a q) -> k a q", q=64)
                                    P.dma("sp", dstap, srcap, dst=btB, src=tabB[i])
                            tabs.append((bt, btB))
                        jobs.append(dict(qc0=qt * 512, nq=512, ktiles=[8, 9, 10, 11] + wtiles,
                                         bias={"wt": {kt: wi for wi, kt in enumerate(wtiles)}, "tab": tabs}))
                    run_batch(jobs, 0, 1024)
        ulist = list(range(8) if units is None else units)
        cx = unit_proj(ulist[0], 0)
        done = []
        for n_, u in enumerate(ulist):
            cx_next = unit_proj(ulist[n_ + 1], (n_ + 1) % 2) if n_ + 1 < len(ulist) else None
            unit_core(cx)
            done.append(cx)
            if len(done) == 2 or cx_next is None:
                for t5 in range(2):
                    oc = t5 * 512
                    for d in range(8):
                        p_, pb = rot()
                        P.op("pe", mm_group(p_, [(c_["wo"][:, d * 128:(d + 1) * 128], sets[c_["par"]]["on"][0][:, oc:oc + 512]) for c_ in done]),
                             reads=[c_["woB"] for c_ in done] + [sets[c_["par"]]["on"][1] for c_ in done], writes=[pb])
                        x_update(p_, pb, d, g * 1024 + oc, 512, s, g, xB[g][t5])
                done = []
            cx = cx_next
        if bgs["g"] is not None:
            for _ in bgs["g"]:
                pass
        P.phase_end()

    def gla_group(l, g):
        jl = l // 2
        s = 1
        P.phase_begin()
        st["p"] = PH_BASE
        set_rot([2, 3, 4, 5, 6, 7])
        norm_mod(s, g)
        P.barrier()
        st["p"] = NORM_BASE
        qT_, qTB = T([128, 1024], BF16, "gq")
        kT_, kTB = T([128, 1024], BF16, "gk")
        ktm, ktmB = T([128, 8, 128], BF16, "ktm")
        vtm, vtmB = T([128, 8, 256], BF16, "vtm")
        sgtm, sgtmB = T([128, 8, 256], BF16, "sgtm")
        osv, osvB = T([128, 8, 256], F32, "osv")
        S_ = alloc([128, 2, 256], F32, "S")
        SB = [Buf("S0"), Buf("S1")]
        Sbf = alloc([128, 2, 256], BF16, "Sbf")
        SbfB = [Buf("Sbf0"), Buf("Sbf1")]
        dT = [T([16, 1024], BF16, f"dT{i_}") for i_ in range(2)]
        wgup, wgupB = T([16, 2, 128], BF16, "wgup")
        bgup, bgupB = T([1, 2, 128], BF16, "bgup")
        ones1, ones1B = T([1, 128], BF16, "ones1")
        gnrow, gnrowB = T([1, 256], F32, "gnrow")
        gnb, gnbB = T([128, 256], F32, "gnb")
        oT, oTB = T([128, 2, 1024], BF16, "oT")
        wo, woB = T([128, 2, 1024], BF16, "gwo")
        prep = []
        for d_ in range(2):
            pr = {}
            pr["qin"] = T([128, 8, 128], BF16, f"qin{d_}")
            pr["kin"] = T([128, 8, 128], BF16, f"kin{d_}")
            pr["kout"] = T([128, 8, 128], BF16, f"kout{d_}")
            pr["atm"] = T([128, 8, 128], BF16, f"atm{d_}")
            pr["dec"] = T([128, 8], F32, f"dec{d_}")
            prep.append(pr)
        spt, sptB = T([128, 4, 128], F32, "spt")
        ext, extB = T([128, 512], F32, "ext")
        E1, E1B = T([128, 4, 128], F32, "E1")
        E2, E2B = T([128, 4, 128], F32, "E2")
        E3, E3B = T([128, 4, 128], F32, "E3")
        otmps = []
        for i_ in range(3):
            otmps.append({"sq": T([128, 256], F32, f"osq{i_}"), "ss": T([128, 1], F32, f"oss{i_}"), "rl": T([128, 1], F32, f"orl{i_}"),
                          "rs": T([128, 1], F32, f"ors{i_}"), "n1": T([128, 256], F32, f"on1{i_}"), "n2": T([128, 256], F32, f"on2{i_}")})
        if l == 1:
            print(f"[sbuf] gla g={g} used up to {st['p']} of {SB_HI} (free {SB_HI - st['p']})")
        P.dma("pool", ones1, K["k_ones"][0:1, :], dst=ones1B)
        P.dma("sp", gnrow, I["g_gn"][jl].rearrange("(o n) -> o n", o=1), dst=gnrowB)
        p_, pb = rot()
        P.op("pe", mm_one(p_[:, 0:256], onesf[0:1, :], gnrow[0:1, :], True, True), reads=[onesfB, gnrowB], writes=[pb])
        P.op("dve", cp(gnb, p_[:, 0:256]), reads=[pb], writes=[gnbB])
        win = I["w_gin"][jl].rearrange("(k p) n -> p k n", p=128)
        w_, wB = nxt("wst", wst)
        P.dma("pool", w_[:, :, 0:32], win[:, :, 3072:3104], dst=wB)
        for t in range(2):
            c0 = g * 1024 + t * 512
            for d_ in range(2):
                p_, pb = rot()
                P.op("pe", mm_group(p_[0:16, :], [(w_[:, k, d_ * 16:(d_ + 1) * 16], hT[:, k, c0:c0 + 512]) for k in range(8)]),
                     reads=[wB, hB[g][t]], writes=[pb])
                P.op("act", act(dT[d_][0][:, t * 512:(t + 1) * 512], p_[0:16, :], AF.Copy), reads=[pb], writes=[dT[d_][1]])

        for h in (range(4) if units is None else units):
            wa, waB = nxt("wst", wst)
            P.dma("pool", wa[:, :, 0:128], win[:, :, 128 * h:128 * h + 128], dst=waB)
            P.dma("pool", wa[:, :, 128:256], win[:, :, 512 + 128 * h:512 + 128 * h + 128], dst=waB)
            wb_, wbB = nxt("wst", wst)
            P.dma("pool", wb_[:, :, 0:256], win[:, :, 1024 + 256 * h:1024 + 256 * h + 256], dst=wbB)
            P.dma("pool", wb_[:, :, 256:512], win[:, :, 2048 + 256 * h:2048 + 256 * h + 256], dst=wbB)
            P.dma("pool", wo, I["w_gout"][jl][256 * h:256 * h + 256, :].rearrange("(c p) n -> p c n", p=128), dst=woB)
            P.dma("pool", wgup, I["w_gup"][jl][:, :, 128 * h:128 * h + 128].rearrange("d r n -> r d n"), dst=wgupB)
            P.dma("pool", bgup, I["b_gup"][jl][:, 128 * h:128 * h + 128].rearrange("(o d) n -> o d n", o=1), dst=bgupB)
            for t in range(2):
                c0 = g * 1024 + t * 512
                p_, pb = rot()
                P.op("pe", mm_group(p_, [(wa[:, k, 0:128], hT[:, k, c0:c0 + 512]) for k in range(8)]), reads=[waB, hB[g][t]], writes=[pb])
                P.op("act", act(qT_[:, t * 512:(t + 1) * 512], p_, AF.Copy), reads=[pb], writes=[qTB])
                p_, pb = rot()
                P.op("pe", mm_group(p_, [(wa[:, k, 128:256], hT[:, k, c0:c0 + 512]) for k in range(8)]), reads=[waB, hB[g][t]], writes=[pb])
                P.op("dve", cp(kT_[:, t * 512:(t + 1) * 512], p_), reads=[pb], writes=[kTB])
                p_, pb = rot()
                for t4 in range(4):
                    cc = c0 + t4 * 128
                    P.op("pe", mm_group(p_[:, t4 * 128:(t4 + 1) * 128], [(hT[:, k, cc:cc + 128], wa[:, k, 128:256]) for k in range(8)]),
                         reads=[waB, hB[g][t]], writes=[pb])
                P.op("act", act(ktm[:, t * 4:(t + 1) * 4, :], p_.rearrange("p (a b) -> p a b", a=4), AF.Copy), reads=[pb], writes=[ktmB])
            def proj_v(tk2):
                pv_, pvB = rot()
                for t2 in range(2):
                    tk = tk2 * 2 + t2
                    cc = g * 1024 + tk * 128
                    P.op("pe", mm_group(pv_[:, t2 * 256:(t2 + 1) * 256], [(hT[:, k, cc:cc + 128], wb_[:, k, 0:256]) for k in range(8)]),
                         reads=[wbB, hB[g][tk // 4]], writes=[pvB])
                P.op("dve", cp(vtm[:, tk2 * 2:tk2 * 2 + 2, :], pv_.rearrange("p (a b) -> p a b", a=2)), reads=[pvB], writes=[vtmB])

            def proj_g(tk2):
                pg_, pgB = rot()
                for t2 in range(2):
                    tk = tk2 * 2 + t2
                    cc = g * 1024 + tk * 128
                    P.op("pe", mm_group(pg_[:, t2 * 256:(t2 + 1) * 256], [(hT[:, k, cc:cc + 128], wb_[:, k, 256:512]) for k in range(8)]),
                         reads=[wbB, hB[g][tk // 4]], writes=[pgB])
                P.op("act", act(sgtm[:, tk2 * 2:tk2 * 2 + 2, :], pg_.rearrange("p (a b) -> p a b", a=2), AF.Silu), reads=[pgB], writes=[sgtmB])

            qv = qT_.rearrange("p (a b) -> p a b", b=128)
            kv = kT_.rearrange("p (a b) -> p a b", b=128)
            bi = 0
            for dr in range(2):
                pr = prep[dr]
                qin, qinB = pr["qin"]
                kin, kinB = pr["kin"]
                kout, koutB = pr["kout"]
                atm, atmB = pr["atm"]
                dec, decB = pr["dec"]
                for hh in range(2):
                    tsl = slice(4 * hh, 4 * hh + 4)
                    p1, p1B = rot()
                    for t4 in range(4):
                        cc = (4 * hh + t4) * 128
                        P.op("pe", mm_group(p1[:, t4 * 128:(t4 + 1) * 128], [(dT[dr][0][0:16, cc:cc + 128], wgup[0:16, dr, :]),
                                                                             (ones1[0:1, :], bgup[0:1, dr, :])]),
                             reads=[dT[dr][1], wgupB, ones1B, bgupB], writes=[p1B])
                    P.op("act", act(ext, p1, AF.Exp, scale=-1.0), reads=[p1B], writes=[extB])
                    P.op("act", act(spt.rearrange("p a b -> p (a b)"), ext, AF.Ln, bias=c_one, scale=1.0), reads=[extB, cstB], writes=[sptB])
                    proj_v(bi)
                    p2, p2B = rot()
                    p3, p3B = rot()
                    for t4 in range(4):
                        P.op("pe", mm_one(p2[:, t4 * 128:(t4 + 1) * 128], spt[:, t4, :], tri[:, 2 + dr, :], True, True), reads=[sptB, triB], writes=[p2B])
                    for t4 in range(4):
                        P.op("pe", mm_one(p3[:, t4 * 128:(t4 + 1) * 128], tri[:, 4 + dr, :], spt[:, t4, :], True, True), reads=[sptB, triB], writes=[p3B])
                    P.op("act", act(E1.rearrange("p a b -> p (a b)"), p2, AF.Exp, bias=c_lnq, scale=1.0), reads=[p2B, cstB], writes=[E1B])
                    P.op("act", act(E2.rearrange("p a b -> p (a b)"), p2, AF.Exp, scale=-1.0), reads=[p2B], writes=[E2B])
                    dcols = p2[:, 127:512:128] if dr == 0 else p2[:, 0:512:128]
                    P.op("act", act(dec[:, 4 * hh:4 * hh + 4], dcols, AF.Exp), reads=[p2B], writes=[decB])
                    P.op("act", act(E3.rearrange("p a b -> p (a b)"), p3, AF.Exp), reads=[p3B], writes=[E3B])
                    P.op("dve", tt(qin[:, tsl, :], qv[:, tsl, :], E1, ALU.mult), reads=[qTB, E1B], writes=[qinB])
                    P.op("pool", tt(kin[:, tsl, :], kv[:, tsl, :], E2, ALU.mult), reads=[kTB, E2B], writes=[kinB])
                    P.op("dve", tt(kout[:, tsl, :], ktm[:, tsl, :], E3, ALU.mult), reads=[ktmB, E3B], writes=[koutB])
                    proj_g(bi)
                    bi += 1
                    p4, p4B = rot()
                    for t4 in range(4):
                        tk = 4 * hh + t4
                        P.op("pe", mm_one(p4[:, t4 * 128:(t4 + 1) * 128], kin[:, tk, :], qin[:, tk, :], True, True),
                             reads=[kinB, qinB], writes=[p4B])
                    for t4 in range(4):
                        tk = 4 * hh + t4
                        P.op("dve", tt(atm[:, tk, :], p4[:, t4 * 128:(t4 + 1) * 128], tri[:, dr, :], ALU.mult), reads=[p4B, triB], writes=[atmB])

            def stage(dr, tk, first_visit):
                pr = prep[dr]
                qin, qinB = pr["qin"]
                kout, koutB = pr["kout"]
                atm, atmB = pr["atm"]
                dec, decB = pr["dec"]
                cc = tk * 128
                oacc, oaccB = ps[dr], psB[dr]
                P.op("pe", mm_one(oacc[:, 0:256], atm[:, tk, :], vtm[:, tk, :], True, False), reads=[atmB, vtmB], writes=[oaccB])
                P.op("pe", mm_one(oacc[:, 0:256], qin[:, tk, :], Sbf[:, dr, :], False, True), reads=[qinB, SbfB[dr]], writes=[oaccB])
                pu, puB = rot()
                P.op("pe", mm_one(pu[:, 0:256], kout[:, tk, :], vtm[:, tk, :], True, True), reads=[koutB, vtmB], writes=[puB])
                dcol = dec[:, tk:tk + 1]
                P.op("dve", stt(Sbf[:, dr, :], S_[:, dr, :], dcol, pu[:, 0:256], ALU.mult, ALU.add),
                     reads=[SB[dr], decB, puB], writes=[SbfB[dr]])
                P.op("dve", stt(S_[:, dr, :], S_[:, dr, :], dcol, pu[:, 0:256], ALU.mult, ALU.add),
                     reads=[SB[dr], decB, puB], writes=[SB[dr]])
                if first_visit:
                    P.op("act", act(osv[:, tk, :], oacc[:, 0:256], AF.Copy), reads=[oaccB], writes=[osvB])
                else:
                    P.op("dve", tt(osv[:, tk, :], oacc[:, 0:256], osv[:, tk, :], ALU.add), reads=[oaccB, osvB], writes=[osvB])

            def out_stage(tk):
                cc = tk * 128
                ot = nxt("otmp", otmps)
                osq, osqB = ot["sq"]
                oss, ossB = ot["ss"]
                orl, orlB = ot["rl"]
                ors, orsB = ot["rs"]
                on1, on1B = ot["n1"]
                on2, on2B = ot["n2"]
                P.op("act", (lambda e, o=osq, i_=osv[:, tk, :], a_=oss: e.activation(out=o, in_=i_, func=AF.Square, accum_out=a_)),
                     reads=[osvB], writes=[osqB, ossB])
                rstd_from(oss, ossB, 1.0 / 256, ors, orsB, orl, orlB)
                P.op("dve", stt(on1, osv[:, tk, :], ors[:, 0:1], gnb, ALU.mult, ALU.mult), reads=[osvB, orsB, gnbB], writes=[on1B])
                P.op("pool", tt(on2, on1, sgtm[:, tk, :], ALU.mult), reads=[on1B, sgtmB], writes=[on2B])
                return on2, on2B

            def out_transpose(tk, on2, on2B):
                cc = tk * 128
                p5, p5B = rot()
                P.op("pe", tr_many([(p5[:, c2 * 128:(c2 + 1) * 128], on2[:, c2 * 128:(c2 + 1) * 128], ident) for c2 in range(2)]),
                     reads=[on2B, identB], writes=[p5B])
                P.op("act", act(oT[:, :, cc:cc + 128], p5[:, 0:256].rearrange("p (a b) -> p a b", a=2), AF.Copy), reads=[p5B], writes=[oTB])

            seqs = [(sq * 2, 2) for sq in range(4)] if g == 0 else [(0, 8)]
            for si, (t0, n) in enumerate(seqs):
                for dr in range(2):
                    if g == 0:
                        P.op("pool", mset(S_[:, dr, :], 0.0), writes=[SB[dr]])
                        P.op("pool", mset(Sbf[:, dr, :], 0.0), writes=[SbfB[dr]])
                    else:
                        P.dma("sp", S_[:, dr, :], I["sgla"][jl, dr, h], dst=SB[dr])
                        P.op("act", act(Sbf[:, dr, :], S_[:, dr, :], AF.Copy), reads=[SB[dr]], writes=[SbfB[dr]])
                for step in range(n):
                    stage(0, t0 + step, step < n // 2)
                    stage(1, t0 + n - 1 - step, step < n // 2)
                if g == 0:
                    for dr in range(2):
                        P.dma("sp", O["ngla"][si, jl, dr, h], S_[:, dr, :], src=SB[dr])
            pend = None
            for tk in range(8):
                cur = (tk,) + out_stage(tk)
                if pend is not None:
                    out_transpose(*pend)
                pend = cur
            out_transpose(*pend)
            for t in range(2):
                for d in range(8):
                    p_, pb = rot()
                    P.op("pe", mm_group(p_, [(wo[:, c2, d * 128:(d + 1) * 128], oT[:, c2, t * 512:(t + 1) * 512]) for c2 in range(2)]),
                         reads=[woB, oTB], writes=[pb])
                    x_update(p_, pb, d, g * 1024 + t * 512, 512, s, g, xB[g][t])
        P.phase_end()

    for l in range(n_layers):
        CUR["l"] = l
        if l == 0 or not do_ffn:
            set_rot(range(1, 8))
            adaln_now(l)
            if l == 0:
                P.phase_end()
        if do_ffn:
            ffn(l, 0, 0)
        if do_mix:
            P.barrier()
            if l % 2 == 0:
                set_rot([4, 5, 6, 7])
                if 1 in groups:
                    attn_build_tab(l // 2)
                for g_ in groups:
                    attn_group(l, g_)
            else:
                for g_ in groups:
                    gla_group(l, g_)
        if do_ffn:
            ffn(l, 1, 2, bg_layer=(l + 1 if (l + 1 < n_layers and not (do_mix and l % 2 == 0 and 0 in groups)) else None),
                bg_cgs=range(36), bg_fin=(0, 1, 2), bg_rate=2)

    P.phase_begin()
    st["p"] = PH_BASE
    set_rot(range(8))
    if dbg:
        for g in range(2):
            for t in range(2):
                c0 = g * 1024 + t * 512
                P.dma("sp", O["dbg"][:, :, c0:c0 + 512], xT[:, :, c0:c0 + 512], src=xB[g][t])
    gfrow, gfrowB = T([1, 1024], F32, "gfrow")
    gfb, gfbB = T([128, 1024], F32, "gfb")
    P.dma("sp", gfrow, I["g_fin"].rearrange("(o n) -> o n", o=1), dst=gfrowB)
    for hf in range(2):
        p_, pb = rot()
        P.op("pe", mm_one(p_, onesf[0:1, :], gfrow[0:1, hf * 512:(hf + 1) * 512], True, True), reads=[onesfB, gfrowB], writes=[pb])
        P.op("dve", cp(gfb[:, hf * 512:(hf + 1) * 512], p_), reads=[pb], writes=[gfbB])
    ytm = [T([128, 1024], F32, f"ytm{i}") for i in range(3)]
    ysqs = [T([128, 1024], F32, f"ysq{i}") for i in range(3)]
    ysts = [(T([128, 1], F32, f"yss{i}"), T([128, 1], F32, f"yrs{i}")) for i in range(3)]
    yo = [T([128, 1024], F32, f"yo{i}") for i in range(3)]
    for tt_ in range(16):
        g, tl = tt_ // 8, tt_ % 8
        c0 = g * 1024 + tl * 128
        yt, ytB = nxt("ytm", ytm)
        for half in range(2):
            p_, pb = rot()
            P.op("pe", tr_many([(p_[:, j * 128:(j + 1) * 128], xT[:, half * 4 + j, c0:c0 + 128], ident) for j in range(4)]),
                 reads=[xB[g][tl // 4], identB], writes=[pb])
            if half == 0:
                P.op("dve", cp(yt[:, 0:512], p_), reads=[pb], writes=[ytB])
            else:
                P.op("act", act(yt[:, 512:1024], p_, AF.Copy), reads=[pb], writes=[ytB])
        ysq, ysqB = nxt("ysq", ysqs)
        (yss, yssB), (yrs, yrsB) = nxt("yst", ysts)
        P.op("act", (lambda e, o=ysq, i_=yt, a_=yss: e.activation(out=o, in_=i_, func=AF.Square, accum_out=a_)),
             reads=[ytB], writes=[ysqB, yssB])
        rstd_from(yss, yssB, 1.0 / D, yrs, yrsB, yrs, yrsB)
        yo_, yoB = nxt("yo", yo)
        P.op("dve", stt(yo_, yt, yrs[:, 0:1], gfb, ALU.mult, ALU.mult), reads=[ytB, yrsB, gfbB], writes=[yoB])
        dsto = (O["yp"] if g == 0 else O["ys"])[tl * 128:(tl + 1) * 128, :]
        P.dma("sp", dsto, yo_, src=yoB)

    import contextlib
    with contextlib.ExitStack() as es:
        sems = [es.enter_context(nc.semaphore(f"s{i}")) for i in range(P.nsem)]
        block = es.enter_context(nc.Block())

        @block.tensor
        def _(e):
            P.replay("pe", e, sems)

        @block.scalar
        def _(e):
            P.replay("act", e, sems)

        @block.vector
        def _(e):
            P.replay("dve", e, sems)

        @block.gpsimd
        def _(e):
            P.replay("pool", e, sems)

        @block.sync
        def _(e):
            P.replay("sp", e, sems)
            for sm, tot in P.semtot.items():
                if tot > 0:
                    e.wait_ge(sems[sm], tot)
            for en in ("pe", "act", "dve", "pool"):
                if P.cnt[en] > 0:
                    e.wait_ge(sems[P.semid(en)], P.cnt[en])
    return nc, P


def make_in_map(inp, core, consts):
    c = core
    f = lambda a: np.ascontiguousarray(np.asarray(a, dtype=np.float32))
    m = {
        "xp": f(inp["x_prompt"][4 * c:4 * c + 4]).reshape(1024, 1024),
        "xs": f(inp["x_sample"][c]).reshape(1024, 1024),
        "cak": f(inp["cache_a_k"][c]).reshape(2, 512, 256),
        "cav": f(inp["cache_a_v"][c]).reshape(2, 512, 256),
        "cbk": f(inp["cache_b_k"][c]).reshape(2, 512, 512),
        "cbv": f(inp["cache_b_v"][c]).reshape(2, 512, 512),
        "sgla": f(inp["state_gla"][c]),
        "cond": f(np.stack([np.asarray(inp["c_ctx"]), np.asarray(inp["c"])[c]])),
        "w_mod": f(inp["w_mod"]), "b_mod": f(inp["b_mod"]), "g_norm": f(inp["g_norm"]),
        "wg": f(inp["w_ffn_gate"]), "wu": f(inp["w_ffn_up"]), "wd": f(inp["w_ffn_down"]),
        "w_ain": f(inp["w_attn_in"]), "w_aout": f(inp["w_attn_out"]), "gq": f(inp["g_qnorm"]), "gk": f(inp["g_knorm"]),
        "relb": f(inp["na_rel_bias"]),
        "w_gin": f(inp["w_gla_in"]), "w_gup": f(inp["w_gla_gup"]), "b_gup": f(inp["b_gla_gup"]), "g_gn": f(inp["g_gla_norm"]),
        "w_gout": f(inp["w_gla_out"]), "g_fin": f(inp["g_final"]),
    }
    m.update(consts)
    return m


_CACHE = {}


def kernel(**inputs):
    if "nc" not in _CACHE:
        _CACHE["nc"] = build_program()[0]
        _CACHE["consts"] = _constants()
    nc = _CACHE["nc"]
    consts = _CACHE["consts"]
    in_maps = [make_in_map(inputs, c, consts) for c in range(N_CORES)]
    res = run_bass_kernel_spmd(nc, in_maps, core_ids=list(range(N_CORES)))
    R = res.results
    y_prompt = np.concatenate([R[c]["yp"].reshape(4, 256, 1024) for c in range(N_CORES)], axis=0)
    y_sample = np.stack([R[c]["ys"].reshape(1024, 1024) for c in range(N_CORES)], axis=0)
    nak = np.concatenate([R[c]["nak"].reshape(4, 2, 256, 4, 64) for c in range(N_CORES)], axis=0)
    nav = np.concatenate([R[c]["nav"].reshape(4, 2, 256, 4, 64) for c in range(N_CORES)], axis=0)
    nbk = np.concatenate([R[c]["nbk"].reshape(4, 2, 256, 8, 64) for c in range(N_CORES)], axis=0)
    nbv = np.concatenate([R[c]["nbv"].reshape(4, 2, 256, 8, 64) for c in range(N_CORES)], axis=0)
    ngla = np.concatenate([R[c]["ngla"].reshape(4, 2, 2, 4, 128, 256) for c in range(N_CORES)], axis=0)
    return (y_prompt.astype(np.float32), y_sample.astype(np.float32), nak.astype(np.float32), nav.astype(np.float32),
            nbk.astype(np.float32), nbv.astype(np.float32), ngla.astype(np.float32))
```
